# Optimizing a Trainium2 kernel written in Bass

```python
import math
import jax, jax.numpy as jnp
from jax import lax
import numpy as np

D_MODEL = 1024
BATCH = 32
SEQ = 2048
DEPTH = 4
DEC_BATCH = 8
DEC_SEQ = 32
PAST_LEN = 4096

CHUNK = 64
N_MIXERS = 2
N_SSM = (DEPTH + 1) // 2
N_SB = DEPTH // 2
EXPAND = 2
SSM_WIDTH = EXPAND * D_MODEL
SSM_GROUP = 16
SSM_GROUPS = SSM_WIDTH // SSM_GROUP
SSM_STATE = 64
SB_WIDTH = EXPAND * D_MODEL
SB_HEAD_DIM = 128
SB_HEADS = SB_WIDTH // SB_HEAD_DIM
Q_BLOCK = 128
RMS_EPS = 1e-6
DT_MIN = 1e-3
DT_MAX = 1e-1

kernel_name = "streaming_s5_stickbreaking_hybrid_step"


def rms_norm(x, g):
    xf = x.astype(jnp.float32)
    y = xf * lax.rsqrt(jnp.mean(xf * xf, axis=-1, keepdims=True) + RMS_EPS)
    return (y * g.astype(jnp.float32)).astype(x.dtype)


def s5_discretize(a_re, a_im, log_step):
    f32 = jnp.float32
    ar = a_re.astype(f32)
    ai = a_im.astype(f32)
    dt = jnp.exp(log_step.astype(f32))[:, None]
    mag = jnp.exp(ar * dt)
    ph = ai * dt
    lb_re = mag * jnp.cos(ph)
    lb_im = mag * jnp.sin(ph)
    nr = lb_re - 1.0
    ni = lb_im
    den = ar * ar + ai * ai
    fac_re = (nr * ar + ni * ai) / den
    fac_im = (ni * ar - nr * ai) / den
    return lb_re, lb_im, fac_re, fac_im


def _ssm_combine(e1, e2):
    a1r, a1i, b1r, b1i = e1
    a2r, a2i, b2r, b2i = e2
    ar = a2r * a1r - a2i * a1i
    ai = a2r * a1i + a2i * a1r
    br = a2r * b1r - a2i * b1i + b2r
    bi = a2r * b1i + a2i * b1r + b2i
    return ar, ai, br, bi


def s5_mixer(u, h0_re, h0_im, a_re, a_im, log_step, b_re, b_im, c_re, c_im, d_skip):
    f32 = jnp.float32
    n, t, _ = u.shape
    blk = min(CHUNK, t)
    nblk = t // blk
    lb_re, lb_im, fac_re, fac_im = s5_discretize(a_re, a_im, log_step)
    br = b_re.astype(f32)
    bi = b_im.astype(f32)
    bbar_re = fac_re[..., None] * br - fac_im[..., None] * bi
    bbar_im = fac_re[..., None] * bi + fac_im[..., None] * br
    cr = c_re.astype(f32)
    ci = c_im.astype(f32)
    uf = u.astype(f32)
    ug = uf.reshape(n, nblk, blk, SSM_GROUPS, SSM_GROUP).transpose(1, 0, 2, 3, 4)

    def step(carry, u_c):
        hr, hi = carry
        bu_re = jnp.einsum('nlgh,gph->nlgp', u_c, bbar_re)
        bu_im = jnp.einsum('nlgh,gph->nlgp', u_c, bbar_im)
        bu_re = bu_re.at[:, 0].add(lb_re * hr - lb_im * hi)
        bu_im = bu_im.at[:, 0].add(lb_re * hi + lb_im * hr)
        a_r = jnp.broadcast_to(lb_re, bu_re.shape)
        a_i = jnp.broadcast_to(lb_im, bu_im.shape)
        _, _, sr, si = lax.associative_scan(_ssm_combine, (a_r, a_i, bu_re, bu_im), axis=1)
        y = jnp.einsum('nlgp,ghp->nlgh', sr, cr) - jnp.einsum('nlgp,ghp->nlgh', si, ci)
        return (sr[:, -1], si[:, -1]), y

    (hr, hi), ys = lax.scan(step, (h0_re.astype(f32), h0_im.astype(f32)), ug)
    y = ys.transpose(1, 0, 2, 3, 4).reshape(n, t, SSM_WIDTH) + d_skip.astype(f32) * uf
    return y.astype(u.dtype), hr, hi


def ssm_layer(x, h0_re, h0_im, g_pre, g_post, w_in, a_re, a_im, log_step,
              b_re, b_im, c_re, c_im, d_skip, w_glu, w_out):
    h = rms_norm(x, g_pre)
    proj = h @ w_in
    u, gate = jnp.split(proj, 2, axis=-1)
    y, hr, hi = s5_mixer(u, h0_re, h0_im, a_re, a_im, log_step, b_re, b_im, c_re, c_im, d_skip)
    y = jax.nn.gelu(y)
    y = y * jax.nn.sigmoid(y @ w_glu)
    o = (y * jax.nn.silu(gate)) @ w_out
    return x + rms_norm(o, g_post), hr, hi


def stick_breaking(q, k, v, q_offset):
    f32 = jnp.float32
    n, hh, t, dh = q.shape
    s = k.shape[2]
    blk = min(Q_BLOCK, t)
    nblk = t // blk
    kf = k.astype(f32)
    vf = v.astype(f32)
    kpos = jnp.arange(s)
    qb = (q.astype(f32) * (dh ** -0.5)).reshape(n, hh, nblk, blk, dh).transpose(2, 0, 1, 3, 4)
    qpos = (q_offset + jnp.arange(t)).reshape(nblk, blk)

    def one_block(args):
        qi, pi = args
        z = jnp.einsum('nhqd,nhkd->nhqk', qi, kf)
        valid = kpos[None, :] < pi[:, None]
        log_keep = jnp.where(valid, jax.nn.log_sigmoid(-z), 0.0)
        log_rest = lax.cumsum(log_keep, axis=3, reverse=True) - log_keep
        w = jnp.where(valid, jnp.exp(jax.nn.log_sigmoid(z) + log_rest), 0.0)
        return jnp.einsum('nhqk,nhkd->nhqd', w, vf)

    o = lax.map(one_block, (qb, qpos))
    return o.transpose(1, 2, 0, 3, 4).reshape(n, hh, t, dh).astype(q.dtype)


def sb_layer(x, k_past, v_past, g_pre, g_post, w_in, w_out):
    n, t, _ = x.shape
    h = rms_norm(x, g_pre)
    proj = h @ w_in
    q, k, v, gate = jnp.split(proj, 4, axis=-1)

    def heads(a):
        return a.reshape(n, t, SB_HEADS, SB_HEAD_DIM).transpose(0, 2, 1, 3)

    q, k, v = heads(q), heads(k), heads(v)
    past = k_past.shape[2]
    k_all = jnp.concatenate([k_past.astype(k.dtype), k], axis=2)
    v_all = jnp.concatenate([v_past.astype(v.dtype), v], axis=2)
    o = stick_breaking(q, k_all, v_all, past)
    o = o.transpose(0, 2, 1, 3).reshape(n, t, SB_WIDTH)
    o = (o * jax.nn.silu(gate)) @ w_out
    return x + rms_norm(o, g_post), k, v


def run_trunk(x, h0_re, h0_im, k_past, v_past, norm_pre, norm_post, w_in_ssm, ssm_a_re, ssm_a_im,
              ssm_log_step, ssm_b_re, ssm_b_im, ssm_c_re, ssm_c_im, ssm_d, w_glu, w_out_ssm,
              w_in_sb, w_out_sb):
    new_k, new_v, new_re, new_im = [], [], [], []
    for i in range(DEPTH):
        j = i // N_MIXERS
        if i % N_MIXERS == 0:
            x, hr, hi = ssm_layer(x, h0_re[j], h0_im[j], norm_pre[i], norm_post[i], w_in_ssm[j],
                                  ssm_a_re[j], ssm_a_im[j], ssm_log_step[j], ssm_b_re[j], ssm_b_im[j],
                                  ssm_c_re[j], ssm_c_im[j], ssm_d[j], w_glu[j], w_out_ssm[j])
            new_re.append(hr)
            new_im.append(hi)
        else:
            x, k, v = sb_layer(x, k_past[j], v_past[j], norm_pre[i], norm_post[i], w_in_sb[j], w_out_sb[j])
            new_k.append(k)
            new_v.append(v)
    return x, jnp.stack(new_k), jnp.stack(new_v), jnp.stack(new_re), jnp.stack(new_im)


def setup_inputs(seed: int = 0) -> dict:
    key = jax.random.key(seed)
    ks = jax.random.split(key, 24)
    f32 = jnp.float32
    nrm = lambda k, shape, s: jax.random.normal(k, shape, f32) * s
    n_idx = jnp.arange(SSM_STATE, dtype=f32)
    a_re = -0.5 + nrm(ks[0], (N_SSM, SSM_GROUPS, SSM_STATE), 0.01)
    a_im = math.pi * n_idx + nrm(ks[1], (N_SSM, SSM_GROUPS, SSM_STATE), 0.01)
    log_step = jax.random.uniform(ks[2], (N_SSM, SSM_GROUPS), f32, math.log(DT_MIN), math.log(DT_MAX))
    return {
        "x_prompt": nrm(ks[3], (BATCH, SEQ, D_MODEL), 1.0),
        "x_sample": nrm(ks[4], (DEC_BATCH, DEC_SEQ, D_MODEL), 1.0),
        "cache_sb_k": nrm(ks[5], (N_SB, DEC_BATCH, SB_HEADS, PAST_LEN, SB_HEAD_DIM), 1.0),
        "cache_sb_v": nrm(ks[6], (N_SB, DEC_BATCH, SB_HEADS, PAST_LEN, SB_HEAD_DIM), 1.0),
        "state_ssm_re": nrm(ks[7], (N_SSM, DEC_BATCH, SSM_GROUPS, SSM_STATE), 0.1),
        "state_ssm_im": nrm(ks[8], (N_SSM, DEC_BATCH, SSM_GROUPS, SSM_STATE), 0.1),
        "norm_pre": 1.0 + nrm(ks[9], (DEPTH, D_MODEL), 0.02),
        "norm_post": 1.0 + nrm(ks[10], (DEPTH, D_MODEL), 0.02),
        "w_in_ssm": nrm(ks[11], (N_SSM, D_MODEL, 2 * SSM_WIDTH), D_MODEL ** -0.5),
        "ssm_a_re": a_re,
        "ssm_a_im": a_im,
        "ssm_log_step": log_step,
        "ssm_b_re": nrm(ks[12], (N_SSM, SSM_GROUPS, SSM_STATE, SSM_GROUP), (2 * SSM_GROUP) ** -0.5),
        "ssm_b_im": nrm(ks[13], (N_SSM, SSM_GROUPS, SSM_STATE, SSM_GROUP), (2 * SSM_GROUP) ** -0.5),
        "ssm_c_re": nrm(ks[14], (N_SSM, SSM_GROUPS, SSM_GROUP, SSM_STATE), (2 * SSM_STATE) ** -0.5),
        "ssm_c_im": nrm(ks[15], (N_SSM, SSM_GROUPS, SSM_GROUP, SSM_STATE), (2 * SSM_STATE) ** -0.5),
        "ssm_d": nrm(ks[16], (N_SSM, SSM_WIDTH), 0.5),
        "w_glu": nrm(ks[17], (N_SSM, SSM_WIDTH, SSM_WIDTH), SSM_WIDTH ** -0.5),
        "w_out_ssm": nrm(ks[18], (N_SSM, SSM_WIDTH, D_MODEL), SSM_WIDTH ** -0.5),
        "w_in_sb": nrm(ks[19], (N_SB, D_MODEL, 4 * SB_WIDTH), D_MODEL ** -0.5),
        "w_out_sb": nrm(ks[20], (N_SB, SB_WIDTH, D_MODEL), SB_WIDTH ** -0.5),
    }


def reference(x_prompt, x_sample, cache_sb_k, cache_sb_v, state_ssm_re, state_ssm_im,
              norm_pre, norm_post, w_in_ssm, ssm_a_re, ssm_a_im, ssm_log_step, ssm_b_re, ssm_b_im,
              ssm_c_re, ssm_c_im, ssm_d, w_glu, w_out_ssm, w_in_sb, w_out_sb):
    nb = x_prompt.shape[0]
    h0_re = jnp.zeros((N_SSM, nb, SSM_GROUPS, SSM_STATE), jnp.float32)
    h0_im = jnp.zeros((N_SSM, nb, SSM_GROUPS, SSM_STATE), jnp.float32)
    kv0 = jnp.zeros((N_SB, nb, SB_HEADS, 0, SB_HEAD_DIM), x_prompt.dtype)
    y_prompt, k_prompt, v_prompt, ssm_re_prompt, ssm_im_prompt = run_trunk(
        x_prompt, h0_re, h0_im, kv0, kv0, norm_pre, norm_post, w_in_ssm, ssm_a_re, ssm_a_im,
        ssm_log_step, ssm_b_re, ssm_b_im, ssm_c_re, ssm_c_im, ssm_d, w_glu, w_out_ssm, w_in_sb, w_out_sb)
    y_sample, k_sample, v_sample, ssm_re_sample, ssm_im_sample = run_trunk(
        x_sample, state_ssm_re, state_ssm_im, cache_sb_k, cache_sb_v, norm_pre, norm_post, w_in_ssm,
        ssm_a_re, ssm_a_im, ssm_log_step, ssm_b_re, ssm_b_im, ssm_c_re, ssm_c_im, ssm_d, w_glu,
        w_out_ssm, w_in_sb, w_out_sb)
    return (y_prompt, y_sample, k_prompt, v_prompt, ssm_re_prompt, ssm_im_prompt,
            k_sample, v_sample, ssm_re_sample, ssm_im_sample)
```

```python
import contextlib
import numpy as np
import ml_dtypes
import concourse.bass as bass
import concourse.mybir as mybir
from concourse.bass_utils import run_bass_kernel_spmd

F32 = mybir.dt.float32
BF16 = mybir.dt.bfloat16
I32 = mybir.dt.int32
ALU = mybir.AluOpType
AF = mybir.ActivationFunctionType

NCORES = 8
DM = 1024
TP = 2048
TS = 32
PAST = 4096
NPS = 4
EPS = 1e-6
TWO_PI = float(2 * np.pi)


class Buf:
    __slots__ = ("name", "lw", "rd", "x")

    def __init__(self, name="", x=False):
        self.name = name
        self.lw = None
        self.rd = {}
        self.x = x


class Sched:
    NSLOT = 6

    def __init__(self, nc, stack):
        self.nc = nc
        self.names = ["pe", "act", "dve", "pool", "sp"]
        self.prog = {e: [] for e in self.names}
        self.sem = {e: stack.enter_context(nc.semaphore("s_" + e)) for e in self.names}
        self.cnt = {e: 0 for e in self.names}
        self.seen = {e: {} for e in self.names}
        self.slots = {}
        for e in ["sp", "act", "pool"]:
            self.slots[e] = [[stack.enter_context(nc.semaphore("d_%s%d" % (e, i))), 0] for i in range(self.NSLOT)]
        self.slot_i = {e: 0 for e in self.slots}

    def _waits(self, eng, deps):
        out = []
        seen = self.seen[eng]
        for (sem, val) in deps:
            k = id(sem)
            if seen.get(k, 0) >= val:
                continue
            seen[k] = val
            out.append((sem, val))
        return out

    def _deps(self, eng, reads, writes):
        deps = []
        own = self.sem[eng]
        for b in reads:
            if b.lw is not None:
                deps.append(b.lw)
        for b in writes:
            if b.lw is not None:
                deps.append(b.lw)
            deps.extend(b.rd.values())
        if eng == "pe":
            deps = [d for d in deps if d[0] is not own]
        return deps

    def op(self, eng, fn, reads=(), writes=()):
        writes = list(writes) + [b for b in reads if b.x]
        reads = [b for b in reads if not b.x]
        waits = self._waits(eng, self._deps(eng, reads, writes))
        self.cnt[eng] += 1
        c = self.cnt[eng]
        sem = self.sem[eng]
        self.prog[eng].append((waits, fn, sem, 1))
        k = id(sem)
        for b in reads:
            if b.rd.get(k, (None, 0))[1] < c:
                b.rd[k] = (sem, c)
        for b in writes:
            b.lw = (sem, c)
            b.rd = {}

    def dma(self, eng, out, in_, reads=(), writes=()):
        slots = self.slots[eng]
        i = self.slot_i[eng]
        self.slot_i[eng] = (i + 1) % len(slots)
        sl = slots[i]
        deps = self._deps(eng, reads, writes)
        if sl[1] > 0:
            deps.append((sl[0], sl[1]))
        waits = self._waits(eng, deps)
        sl[1] += 16
        dsem, dval = sl[0], sl[1]
        self.prog[eng].append((waits, (lambda e, o=out, i_=in_: e.dma_start(out=o, in_=i_)), dsem, 16))
        for b in reads:
            b.rd[id(dsem)] = (dsem, dval)
        for b in writes:
            b.lw = (dsem, dval)
            b.rd = {}

    def barrier(self):
        deps = [(self.sem[e], self.cnt[e]) for e in self.names if self.cnt[e] > 0]
        for e, slots in self.slots.items():
            deps += [(s[0], s[1]) for s in slots if s[1] > 0]
        for e in self.names:
            own = self.sem[e]
            waits = self._waits(e, [d for d in deps if not (e == "pe" and d[0] is own)])
            if waits:
                self.prog[e].append((waits, None, None, 0))

    def finish(self):
        self.barrier()

    def emit(self, block):
        def run(engname):
            def f(e):
                for (waits, fn, sem, inc) in self.prog[engname]:
                    for (s, v) in waits:
                        e.wait_ge(s, v)
                    if fn is not None:
                        fn(e).then_inc(sem, inc)
            return f
        block.tensor(run("pe"))
        block.scalar(run("act"))
        block.vector(run("dve"))
        block.gpsimd(run("pool"))
        block.sync(run("sp"))


class Prog:
    def __init__(self, seqs=None):
        self.nc = bass.Bass("TRN2", target_bir_lowering=False)
        self.st = contextlib.ExitStack()
        self.seqs = list(range(NPS + 1)) if seqs is None else seqs

    def din(self, n, s):
        return self.nc.dram_tensor(n, list(s), F32, kind="ExternalInput").ap()

    def dout(self, n, s):
        return self.nc.dram_tensor(n, list(s), F32, kind="ExternalOutput").ap()

    def sb(self, n, s, dt=F32):
        return self.st.enter_context(self.nc.sbuf_tensor(n, list(s), dt))

    def dump(self, name, ap, bufs, shape):
        if "D" not in getattr(self, "dbg", ""):
            return
        if not hasattr(self, "dumps"):
            self.dumps = {}
        if name in self.dumps:
            return
        o = self.nc.dram_tensor("dbg_" + name, list(shape), F32, kind="ExternalOutput").ap()
        self.dumps[name] = o
        stg = self.sb("dstg_" + name, list(shape))
        b = Buf("dstg")
        self.cp(stg[:], ap, bufs, [b])
        self.S.dma("sp", o[:, :], stg[:], reads=[b])

    def a_reset(self, mark=0):
        self.apos = mark

    def a16(self, n):
        n = (n + 1) // 2 * 2
        ap = self.AR[:, self.apos:self.apos + n]
        self.apos += n
        assert self.apos <= self.NA, ("arena overflow", self.apos, self.NA)
        return ap

    def a32(self, n):
        return self.a16(2 * n).bitcast(F32)

    def ai32(self, n):
        return self.a16(2 * n).bitcast(I32)

    def mm(self, out, lhsT, rhs, start, stop, r, w):
        self.S.op("pe", lambda e: e.matmul(out, lhsT, rhs, start=start, stop=stop), r, w)

    def tr(self, out, in_, ident, r, w):
        self.S.op("pe", lambda e: e.transpose(out, in_, ident), r, w)

    def act(self, out, in_, func, r, w, bias=None, scale=None, accum=None):
        kw = {}
        if bias is not None:
            kw["bias"] = bias
        if scale is not None:
            kw["scale"] = scale
        if accum is not None:
            kw["accum_out"] = accum
        self.S.op("act", lambda e: e.activation(out, in_, func, **kw), r, w)

    def tt(self, out, a, b, op, r, w, eng="dve"):
        self.S.op(eng, lambda e: e.tensor_tensor(out, a, b, op), r, w)

    def ts(self, out, a, s1, s2, op0, op1, r, w, eng="dve"):
        if s2 is None:
            self.S.op(eng, lambda e: e.tensor_scalar(out, a, s1, None, op0), r, w)
        else:
            self.S.op(eng, lambda e: e.tensor_scalar(out, a, s1, s2, op0, op1), r, w)

    def stt(self, out, in0, scalar, in1, op0, op1, r, w):
        self.S.op("dve", lambda e: e.scalar_tensor_tensor(out, in0, scalar, in1, op0, op1), r, w)

    def cp(self, out, in_, r, w, eng="dve"):
        if eng == "act":
            self.S.op("act", lambda e: e.copy(out, in_), r, w)
        else:
            self.S.op(eng, lambda e: e.tensor_copy(out, in_), r, w)

    def scan(self, out, d0, d1, init, r, w):
        self.S.op("dve", lambda e: e.tensor_tensor_scan(out, d0, d1, init, ALU.mult, ALU.add), r, w)

    def recip(self, out, in_, r, w):
        self.S.op("dve", lambda e: e.reciprocal(out, in_), r, w)

    def memset(self, ap, v, w, eng="dve"):
        self.S.op(eng, lambda e: e.memset(ap, v), (), w)

    def build(self):
        nc = self.nc
        self.S = Sched(nc, self.st)
        S = self.S
        d = self.din
        self.xp = d("xp", [NPS, TP, DM]); self.xs = d("xs", [TS, DM])
        self.ck = d("ck", [2, 16, PAST, 128]); self.cv = d("cv", [2, 16, PAST, 128])
        self.sre = d("sre", [2, 128, 64]); self.sim = d("sim", [2, 128, 64])
        self.npre = d("npre", [4, DM]); self.npost = d("npost", [4, DM])
        self.wis = d("wis", [2, DM, 4096]); self.are = d("are", [2, 128, 64]); self.aim = d("aim", [2, 128, 64])
        self.lst = d("lst", [2, 128]); self.bre = d("bre", [2, 128, 64, 16]); self.bim = d("bim", [2, 128, 64, 16])
        self.cre = d("cre", [2, 128, 16, 64]); self.cim = d("cim", [2, 128, 16, 64]); self.dsk = d("dsk", [2, 2048])
        self.wgl = d("wgl", [2, 2048, 2048]); self.wos = d("wos", [2, 2048, DM])
        self.wib = d("wib", [2, DM, 8192]); self.wob = d("wob", [2, 2048, DM])
        c_ident = d("c_ident", [128, 128]); c_negU = d("c_negU", [128, 128]); c_maskP = d("c_maskP", [128, 2048])
        c_iota = d("c_iota", [128, 512]); c_mB = d("c_mB", [128, 2]); c_mC = d("c_mC", [128, 4])
        o = self.dout
        self.yp = o("yp", [NPS, TP, DM]); self.ys = o("ys", [TS, DM])
        self.kp = o("kp", [2, NPS, 16, TP, 128]); self.vp = o("vp", [2, NPS, 16, TP, 128])
        self.srp = o("srp", [2, NPS, 128, 64]); self.sip = o("sip", [2, NPS, 128, 64])
        self.ks = o("ks", [2, 16, TS, 128]); self.vs = o("vs", [2, 16, TS, 128])
        self.srs = o("srs", [2, 128, 64]); self.sis = o("sis", [2, 128, 64])

        sb = self.sb
        self.X = sb("X", [128, 16, DM]); self.bX = [Buf("X%d" % i) for i in range(16)]
        self.identf = sb("identf", [128, 128]); self.identb = sb("identb", [128, 128], BF16)
        self.negU = sb("negU", [128, 128], BF16); self.negOnes = sb("negOnes", [128, 128], BF16)
        self.maskP = sb("maskP", [128, 2048], BF16)
        self.G = sb("G", [128, DM]); self.bG = Buf("G")
        self.cols = sb("cols", [128, 8])
        self.mB = sb("mB", [128, 2]); self.mC = sb("mC", [128, 4])
        self.iotac = sb("iotac", [128, 512])
        self.NA = 67000
        self.AR = sb("arena", [128, self.NA], BF16)
        self.apos = 0
        self.bC = Buf("consts")
        self.PS = [self.st.enter_context(nc.psum_tensor("ps%d" % i, [128, 512], F32)) for i in range(8)]
        self.bPS = [Buf("ps%d" % i, x=True) for i in range(8)]

        tmpf = self.a32(2048)
        bt = Buf("tmpc")
        S.dma("sp", self.identf[:], c_ident[:, :], writes=[self.bC])
        S.dma("sp", self.iotac[:], c_iota[:, :], writes=[self.bC])
        S.dma("sp", self.mB[:], c_mB[:, :], writes=[self.bC])
        S.dma("sp", self.mC[:], c_mC[:, :], writes=[self.bC])
        S.dma("sp", tmpf[:, 0:128], c_negU[:, :], writes=[bt])
        self.cp(self.negU[:], tmpf[:, 0:128], [bt], [self.bC])
        self.cp(self.identb[:], self.identf[:], [self.bC], [self.bC])
        self.memset(self.negOnes[:], -1.0, [self.bC])
        self.memset(self.cols[:, 0:1], EPS, [self.bC])
        self.memset(self.cols[:, 1:2], 1.0, [self.bC])
        self.memset(self.cols[:, 2:3], float(np.pi / 2), [self.bC])
        self.memset(self.cols[:, 3:4], 0.0, [self.bC])
        S.dma("sp", tmpf[:, :], c_maskP[:, :], reads=[], writes=[bt])
        self.cp(self.maskP[:], tmpf[:, :], [bt], [self.bC])
        S.barrier()
        self.a_reset()

        for s in self.seqs:
            self.run_seq(s)
        S.finish()
        with nc.Block() as block:
            S.emit(block)
        return nc

    def run_seq(self, s):
        S = self.S
        smp = (s == NPS)
        T = TS if smp else TP
        tp = min(T, 128)
        ntt = T // tp
        self.T, self.tp, self.ntt, self.smp, self.s = T, tp, ntt, smp, s
        self.tc = min(T, 512)
        self.nch = T // self.tc
        src = self.xs if smp else self.xp[s]
        for tt in range(ntt):
            S.dma("sp", self.X[0:tp, tt, :], src[tt * tp:(tt + 1) * tp, :], writes=[self.bX[tt]])
        import os
        self.dbg = os.environ.get("K_DBG", "")
        nl = int(self.dbg[1]) if self.dbg.startswith("L") else 4
        for layer in range(nl):
            j = layer // 2
            if layer % 2 == 0:
                self.ssm_layer(layer, j)
            else:
                self.sb_layer(layer, j)
        dst = self.ys if smp else self.yp[s]
        for tt in range(ntt):
            S.dma("sp", dst[tt * tp:(tt + 1) * tp, :], self.X[0:tp, tt, :], reads=[self.bX[tt]])

    def load_gain(self, src_row):
        self.S.dma("sp", self.G[:], src_row.partition_broadcast(128), writes=[self.bG])

    def prenorm_tile(self, tt, HT, bHT, col0, scr, bscr):
        tp = self.tp
        sq, ss, rs, hb = scr
        xt = self.X[0:tp, tt, :]
        self.memset(ss[0:tp, 0:4], 0.0, [bscr])
        self.act(sq[0:tp, :], xt, AF.Square, [self.bX[tt]], [bscr], accum=ss[0:tp, 0:1])
        self.act(rs[0:tp, 0:1], ss[0:tp, 0:1], AF.Sqrt, [bscr, self.bC], [bscr], bias=self.cols[0:tp, 0:1], scale=1.0 / DM)
        self.recip(rs[0:tp, 1:2], rs[0:tp, 0:1], [bscr], [bscr])
        self.stt(hb[0:tp, :], xt, rs[0:tp, 1:2], self.G[0:tp, :], ALU.mult, ALU.mult, [self.bX[tt], bscr, self.bG], [bscr])
        pb = self.PS[7][:].bitcast(BF16)
        for kc in range(8):
            self.tr(pb[:, kc * 128:kc * 128 + tp], hb[0:tp, kc * 128:(kc + 1) * 128], self.identb[0:tp, 0:tp],
                    [bscr, self.bC], [self.bPS[7]])
        self.cp(HT[:, :, col0:col0 + tp], pb.rearrange("p (k c) -> p k c", c=128)[:, :, 0:tp], [self.bPS[7]], [bHT], eng="act")

    def outproj_tile(self, tt, ZT, bZT, col0, Wout, bW, scr, bscr):
        tp = self.tp
        sq, ss, rs, tmp = scr
        for half in range(2):
            pbk = self.PS[5 + half]
            for jb in range(16):
                self.mm(pbk[0:tp, :], ZT[:, jb, col0:col0 + tp], Wout[:, jb, half * 512:(half + 1) * 512],
                        jb == 0, jb == 15, [bZT, bW], [self.bPS[5 + half]])
        self.memset(ss[0:tp, 0:4], 0.0, [bscr])
        for half in range(2):
            self.act(sq[0:tp, :], self.PS[5 + half][0:tp, :], AF.Square, [self.bPS[5 + half]], [bscr], accum=ss[0:tp, half:half + 1])
        self.tt(ss[0:tp, 2:3], ss[0:tp, 0:1], ss[0:tp, 1:2], ALU.add, [bscr], [bscr])
        self.act(rs[0:tp, 0:1], ss[0:tp, 2:3], AF.Sqrt, [bscr, self.bC], [bscr], bias=self.cols[0:tp, 0:1], scale=1.0 / DM)
        self.recip(rs[0:tp, 1:2], rs[0:tp, 0:1], [bscr], [bscr])
        for half in range(2):
            xs_ = self.X[0:tp, tt, half * 512:(half + 1) * 512]
            self.stt(tmp[0:tp, :], self.PS[5 + half][0:tp, :], rs[0:tp, 1:2], self.G[0:tp, half * 512:(half + 1) * 512],
                     ALU.mult, ALU.mult, [self.bPS[5 + half], bscr, self.bG], [bscr])
            self.tt(xs_, xs_, tmp[0:tp, :], ALU.add, [bscr, self.bX[tt]], [self.bX[tt]])

    def load_wblk(self, dst, bdst, w2d, c0, kc):
        self.S.dma("pool", dst, w2d[:, c0:c0 + 128].rearrange("(kc k) c -> k kc c", k=128), writes=[bdst])

    def sb_layer(self, layer, j):
        S = self.S
        T, tp, ntt, tc, nch, smp, s = self.T, self.tp, self.ntt, self.tc, self.nch, self.smp, self.s
        S.barrier()
        self.a_reset()
        HT = self.a16(8 * T).rearrange("p (k t) -> p k t", t=T); bHT = Buf("HT")
        OT = self.a16(16 * T).rearrange("p (k t) -> p k t", t=T); bOT = Buf("OT")
        mark = self.apos
        sq = self.a32(DM); ss = self.a32(4); rs = self.a32(2); hb = self.a16(DM); bscr = Buf("scr")
        self.load_gain(self.npre[layer:layer + 1, :])
        for tt in range(ntt):
            self.prenorm_tile(tt, HT, bHT, tt * tp, (sq, ss, rs, hb), bscr)
        S.barrier()
        if "a" in self.dbg:
            return
        self.a_reset(mark)
        wb = [self.a16(8 * 128).rearrange("p (k c) -> p k c", c=128) for _ in range(2)]
        bwb = [Buf("wb0"), Buf("wb1")]
        qT = self.a16(T); kT = self.a16(T); SG = self.a16(T)
        V = self.a16(ntt * 128).rearrange("p (t d) -> p t d", d=128)
        bq, bk, bv, bsg = Buf("qT"), Buf("kT"), Buf("V"), Buf("SG")
        Kst = self.a32(512); Vst = self.a32(512); Kb = self.a16(512)
        bKst, bVst, bKb = Buf("Kst"), Buf("Vst"), Buf("Kb")
        E = self.a32(512); bE = Buf("E")
        SP = [self.a16(512), self.a16(512)]; bSP = [Buf("SP0"), Buf("SP1")]
        SS = self.a16(512); bSS = Buf("SS")
        A = [self.a16(512), self.a16(512)]; bA = [Buf("A0"), Buf("A1")]
        if smp:
            KC = [self.a16(128), self.a16(128)]; bKC = [Buf("KC0"), Buf("KC1")]
            VC = [self.a16(128), self.a16(128)]; bVC = [Buf("VC0"), Buf("VC1")]
            KCT = [self.a16(128), self.a16(128)]; bKCT = [Buf("KCT0"), Buf("KCT1")]
        w2d = self.wib[j]
        kdst = self.ks[j] if smp else self.kp[j, s]
        vdst = self.vs[j] if smp else self.vp[j, s]
        wi = 0
        ntg = (ntt + 3) // 4
        for h in range(16):
            cols = [h * 128, 2048 + h * 128, 4096 + h * 128, 6144 + h * 128]
            for which, c0 in ((0, cols[0]), (3, cols[3])):
                w = wb[wi % 2]; bw = bwb[wi % 2]; wi += 1
                self.load_wblk(w, bw, w2d, c0, 8)
                for ch in range(nch):
                    pbk = self.PS[ch % 2]; bp = self.bPS[ch % 2]
                    for kc in range(8):
                        self.mm(pbk[:, 0:tc], w[:, kc, :], HT[:, kc, ch * tc:(ch + 1) * tc], kc == 0, kc == 7, [bw, bHT], [bp])
                    if which == 0:
                        self.act(qT[:, ch * tc:(ch + 1) * tc], pbk[:, 0:tc], AF.Copy, [bp], [bq], scale=float(128 ** -0.5))
                    else:
                        self.act(SG[:, ch * tc:(ch + 1) * tc], pbk[:, 0:tc], AF.Silu, [bp], [bsg])
            for which, c0 in ((1, cols[1]), (2, cols[2])):
                w = wb[wi % 2]; bw = bwb[wi % 2]; wi += 1
                self.load_wblk(w, bw, w2d, c0, 8)
                for tg in range(ntg):
                    nt_ = min(4, ntt - tg * 4)
                    pbk = self.PS[tg % 2]; bp = self.bPS[tg % 2]
                    for ti in range(nt_):
                        tt = tg * 4 + ti
                        for kc in range(8):
                            self.mm(pbk[0:tp, ti * 128:(ti + 1) * 128], HT[:, kc, tt * tp:(tt + 1) * tp], w[:, kc, :],
                                    kc == 0, kc == 7, [bw, bHT], [bp])
                    ncol = nt_ * 128
                    if which == 1:
                        self.cp(Kst[0:tp, 0:ncol], pbk[0:tp, 0:ncol], [bp], [bKst])
                        self.cp(Kb[0:tp, 0:ncol], pbk[0:tp, 0:ncol], [bp], [bKb], eng="act")
                        S.dma("sp", kdst[h, tg * 4 * tp:(tg * 4 + nt_) * tp, :].rearrange("(t p) d -> p t d", p=tp),
                              Kst[0:tp, 0:ncol].rearrange("p (t d) -> p t d", d=128), reads=[bKst])
                        pb = self.PS[7][:].bitcast(BF16)
                        for ti in range(nt_):
                            self.tr(pb[:, ti * 128:ti * 128 + tp], Kb[0:tp, ti * 128:(ti + 1) * 128], self.identb[0:tp, 0:tp],
                                    [bKb, self.bC], [self.bPS[7]])
                        if tp == 128:
                            self.cp(kT[:, tg * 512:tg * 512 + ncol], pb[:, 0:ncol], [self.bPS[7]], [bk])
                        else:
                            self.cp(kT[:, 0:tp], pb[:, 0:tp], [self.bPS[7]], [bk])
                    else:
                        self.cp(Vst[0:tp, 0:ncol], pbk[0:tp, 0:ncol], [bp], [bVst])
                        self.cp(V[0:tp, tg * 4:tg * 4 + nt_, :], pbk[0:tp, 0:ncol].rearrange("p (t d) -> p t d", d=128), [bp], [bv], eng="act")
                        S.dma("sp", vdst[h, tg * 4 * tp:(tg * 4 + nt_) * tp, :].rearrange("(t p) d -> p t d", p=tp),
                              Vst[0:tp, 0:ncol].rearrange("p (t d) -> p t d", d=128), reads=[bVst])
            if "b" in self.dbg:
                continue
            nqt = nch
            for qi in range(nqt):
                nq = tc
                qs = qT[:, qi * tc:(qi + 1) * tc]
                if smp:
                    blocks = [("cur", 0)] + [("past", b) for b in range(PAST // 128 - 1, -1, -1)]
                else:
                    blocks = [("cur", b) for b in range(4 * qi + 3, -1, -1)]
                acc = self.PS[4 + (qi % 2)] if not smp else self.PS[4]
                bacc = self.bPS[4 + (qi % 2)] if not smp else self.bPS[4]
                nb = len(blocks)
                for bi, (kind, b) in enumerate(blocks):
                    zb = self.PS[2 + bi % 2]; bz = self.bPS[2 + bi % 2]
                    sp_, bsp = SP[bi % 2], bSP[bi % 2]
                    a_, ba = A[bi % 2], bA[bi % 2]
                    mask = None
                    if kind == "past":
                        sl = bi % 2
                        S.dma("pool", KC[sl][:, :], self.ck[j, h, b * 128:(b + 1) * 128, :], writes=[bKC[sl]])
                        S.dma("pool", VC[sl][:, :], self.cv[j, h, b * 128:(b + 1) * 128, :], writes=[bVC[sl]])
                        pb = self.PS[7][:].bitcast(BF16)
                        self.tr(pb[:, 0:128], KC[sl][:, :], self.identb[:, :], [bKC[sl], self.bC], [self.bPS[7]])
                        self.cp(KCT[sl][:, :], pb[:, 0:128], [self.bPS[7]], [bKCT[sl]])
                        nk = 128
                        kTb = KCT[sl][:, :]; rk = [bKCT[sl]]
                        vb = VC[sl][:, :]; rv = [bVC[sl]]
                    else:
                        nk = tp
                        kTb = kT[:, b * tp:(b + 1) * tp]; rk = [bk]
                        vb = V[0:tp, b, :]; rv = [bv]
                        if smp:
                            mask = self.maskP[0:nk, 0:nq]
                        elif b >= 4 * qi:
                            mask = self.maskP[0:nk, (b - 4 * qi) * 512:(b - 4 * qi + 1) * 512]
                    self.mm(zb[0:nk, 0:nq], kTb, qs, True, True, rk + [bq], [bz])
                    self.act(E[0:nk, 0:nq], zb[0:nk, 0:nq], AF.Exp, [bz], [bE])
                    self.act(sp_[0:nk, 0:nq], E[0:nk, 0:nq], AF.Ln, [bE, self.bC], [bsp], bias=self.cols[0:nk, 1:2])
                    if mask is not None:
                        self.tt(sp_[0:nk, 0:nq], sp_[0:nk, 0:nq], mask, ALU.mult, [bsp, self.bC], [bsp])
                    last_acc = (bi == 0)
                    self.mm(zb[0:nk, 0:nq], self.negU[0:nk, 0:nk], sp_[0:nk, 0:nq], False, last_acc, [bsp, self.bC], [bz])
                    if bi > 0:
                        self.mm(zb[0:nk, 0:nq], self.negOnes[0:128, 0:nk], SS[:, 0:nq], False, True, [bSS, self.bC], [bz])
                    self.act(a_[0:nk, 0:nq], zb[0:nk, 0:nq], AF.Exp, [bz], [ba])
                    if mask is not None:
                        self.tt(a_[0:nk, 0:nq], a_[0:nk, 0:nq], mask, ALU.mult, [ba, self.bC], [ba])
                    self.mm(acc[:, 0:nq], vb, a_[0:nk, 0:nq], bi == 0, bi == nb - 1, rv + [ba], [bacc])
                    if bi < nb - 1:
                        if bi == 0:
                            if nk < 128:
                                self.memset(SS[:, 0:nq], 0.0, [bSS])
                            self.cp(SS[0:nk, 0:nq], sp_[0:nk, 0:nq], [bsp], [bSS])
                        else:
                            self.tt(SS[0:nk, 0:nq], SS[0:nk, 0:nq], sp_[0:nk, 0:nq], ALU.add, [bsp, bSS], [bSS])
                if h == 0 and qi == 0:
                    self.dump("acc", acc[:, 0:32], [bacc], [128, 32])
                    self.dump("qT", qT[:, 0:32], [bq], [128, 32])
                    self.dump("kT", kT[:, 0:32], [bk], [128, 32])
                    self.dump("SG", SG[:, 0:32], [bsg], [128, 32])
                    self.dump("SS", SS[:, 0:32], [bSS], [128, 32])
                    self.dump("A", A[0][:, 0:32], [bA[0]], [128, 32])
                    self.dump("SP", SP[0][:, 0:32], [bSP[0]], [128, 32])
                    self.dump("E", E[:, 0:32], [bE], [128, 32])
                self.tt(OT[:, h, qi * tc:(qi + 1) * tc], acc[:, 0:nq], SG[:, qi * tc:(qi + 1) * tc], ALU.mult, [bacc, bsg], [bOT])
        S.barrier()
        if "c" in self.dbg or "b" in self.dbg:
            return
        self.a_reset(0 if 8 * T >= 2100 else mark)
        sq = self.a32(512); ss = self.a32(4); rs = self.a32(2); tmp = self.a32(512); bscr = Buf("scr2")
        self.a_reset(max(mark, self.apos) if 8 * T < 2100 else mark)
        Wout = self.a16(16 * DM).rearrange("p (k c) -> p k c", c=DM); bW = Buf("Wout")
        for half in range(2):
            S.dma("pool", Wout[:, half * 8:(half + 1) * 8, :],
                  self.wob[j, half * 1024:(half + 1) * 1024, :].rearrange("(kc k) c -> k kc c", k=128), writes=[bW])
        self.load_gain(self.npost[layer:layer + 1, :])
        for tt in range(ntt):
            self.outproj_tile(tt, OT, bOT, tt * tp, Wout, bW, (sq, ss, rs, tmp), bscr)

    def ssm_layer(self, layer, j):
        S = self.S
        T, tp, ntt, tc, nch, smp, s = self.T, self.tp, self.ntt, self.tc, self.nch, self.smp, self.s
        S.barrier()
        self.a_reset()
        a16, a32 = self.a16, self.a32
        w3 = lambda n, c: a16(n * c).rearrange("p (f c) -> p f c", c=c)
        WpR = w3(16, 128); WpI = w3(16, 128); WpBR = w3(16, 128); WpBI = w3(16, 128)
        VcLR = w3(32, 32); VcLI = w3(32, 32); VcHR = w3(32, 64); VcHI = w3(32, 64)
        RhoT = a32(64); PhiT = a32(64); DT = a32(16)
        K0r = a32(64); K0i = a32(64); HFr = a32(64); HFi = a32(64)
        bPar = Buf("par"); bK0 = Buf("K0"); bHF = Buf("HF")
        markW = self.apos
        Wout = a16(16 * DM).rearrange("p (k c) -> p k c", c=DM); bW = Buf("Wout")
        mark = self.apos
        self.a_reset(markW)
        for wz in (WpR, WpI, WpBR, WpBI):
            self.memset(wz[64:128, :, :], 0.0, [bPar])
        for wz in (VcHR, VcHI):
            self.memset(wz[:, :, :], 0.0, [bPar])
        NPn = 512
        ar = a32(NPn); ai_ = a32(NPn); dtb = a32(8); rho = a32(NPn); phi = a32(NPn)
        t0 = a32(NPn); t1 = a32(NPn); t2 = a32(NPn); t3 = a32(NPn); t4 = a32(NPn); ti = self.ai32(NPn)
        fr = a32(NPn); fi = a32(NPn)
        bn = Buf("prel")
        P16 = slice(0, 16)
        S.dma("sp", ar[P16, :], self.are[j].rearrange("(f g) p -> f (g p)", g=8), writes=[bn])
        S.dma("sp", ai_[P16, :], self.aim[j].rearrange("(f g) p -> f (g p)", g=8), writes=[bn])
        S.dma("sp", dtb[P16, :], self.lst[j].rearrange("(f g) -> f g", g=8), writes=[bn])
        self.act(dtb[P16, :], dtb[P16, :], AF.Exp, [bn], [bn])
        dtbb = dtb[P16, :].unsqueeze(2).to_broadcast([16, 8, 64])
        v3 = lambda ap: ap[P16, :].rearrange("f (g p) -> f g p", p=64)
        self.tt(v3(t0), v3(ar), dtbb, ALU.mult, [bn], [bn])
        self.act(rho[P16, :], t0[P16, :], AF.Exp, [bn], [bn])
        self.tt(v3(t0), v3(ai_), dtbb, ALU.mult, [bn], [bn])
        self.ts(phi[P16, :], t0[P16, :], float(1.0 / TWO_PI), None, ALU.mult, None, [bn], [bn])
        self.cp(ti[P16, :], phi[P16, :], [bn], [bn])
        self.cp(t1[P16, :], ti[P16, :], [bn], [bn])
        self.tt(t1[P16, :], phi[P16, :], t1[P16, :], ALU.subtract, [bn], [bn])
        self.act(t2[P16, :], t1[P16, :], AF.Sin, [bn, self.bC], [bn], bias=self.cols[P16, 3:4], scale=TWO_PI)
        self.stt(t3[P16, :], t1[P16, :], -1.0, t1[P16, :], ALU.mult, ALU.max, [bn], [bn])
        self.act(t3[P16, :], t3[P16, :], AF.Sin, [bn, self.bC], [bn], bias=self.cols[P16, 2:3], scale=-TWO_PI)
        self.tt(t2[P16, :], t2[P16, :], rho[P16, :], ALU.mult, [bn], [bn])
        self.tt(t3[P16, :], t3[P16, :], rho[P16, :], ALU.mult, [bn], [bn])
        self.ts(t3[P16, :], t3[P16, :], -1.0, None, ALU.add, None, [bn], [bn])
        self.tt(t0[P16, :], ar[P16, :], ar[P16, :], ALU.mult, [bn], [bn])
        self.tt(t1[P16, :], ai_[P16, :], ai_[P16, :], ALU.mult, [bn], [bn])
        self.tt(t0[P16, :], t0[P16, :], t1[P16, :], ALU.add, [bn], [bn])
        self.recip(t0[P16, :], t0[P16, :], [bn], [bn])
        self.tt(t1[P16, :], t3[P16, :], ar[P16, :], ALU.mult, [bn], [bn])
        self.tt(t4[P16, :], t2[P16, :], ai_[P16, :], ALU.mult, [bn], [bn])
        self.tt(t1[P16, :], t1[P16, :], t4[P16, :], ALU.add, [bn], [bn])
        self.tt(fr[P16, :], t1[P16, :], t0[P16, :], ALU.mult, [bn], [bn])
        self.tt(t1[P16, :], t2[P16, :], ar[P16, :], ALU.mult, [bn], [bn])
        self.tt(t4[P16, :], t3[P16, :], ai_[P16, :], ALU.mult, [bn], [bn])
        self.tt(t1[P16, :], t1[P16, :], t4[P16, :], ALU.subtract, [bn], [bn])
        self.tt(fi[P16, :], t1[P16, :], t0[P16, :], ALU.mult, [bn], [bn])
        for (srct, dstt) in ((rho, RhoT), (phi, PhiT)):
            for q4 in range(4):
                self.tr(self.PS[0][:, q4 * 16:(q4 + 1) * 16], srct[P16, q4 * 128:(q4 + 1) * 128], self.identf[P16, P16],
                        [bn, self.bC], [self.bPS[0]])
            self.cp(dstt[:, :], self.PS[0][:, 0:64], [self.bPS[0]], [bPar])
        dn = a32(128)
        S.dma("sp", dn[P16, :], self.dsk[j].rearrange("(f c) -> f c", c=128), writes=[bn])
        self.tr(self.PS[0][:, 0:16], dn[P16, :], self.identf[P16, P16], [bn, self.bC], [self.bPS[0]])
        self.cp(DT[:, :], self.PS[0][:, 0:16], [self.bPS[0]], [bPar])
        markB = self.apos
        Bnr = a32(2048); Bni = a32(2048); Btmp = a32(2048); Bbr = a32(2048); Bbi = a32(2048); Bp = a32(4096)
        bB = Buf("Bn"); bBp = Buf("Bp")
        brv = self.bre[j].rearrange("(f g) p h -> f g (p h)", g=8)
        biv = self.bim[j].rearrange("(f g) p h -> f g (p h)", g=8)
        for q4 in range(4):
            S.dma("sp", Bnr[P16, :].rearrange("f (g x) -> f g x", g=2), brv[:, 2 * q4:2 * q4 + 2, :], writes=[bB])
            S.dma("sp", Bni[P16, :].rearrange("f (g x) -> f g x", g=2), biv[:, 2 * q4:2 * q4 + 2, :], writes=[bB])
            v4 = lambda ap: ap[P16, :].rearrange("f (g p h) -> f (g p) h", g=2, h=16)
            frb = fr[P16, q4 * 128:(q4 + 1) * 128].unsqueeze(2).to_broadcast([16, 128, 16])
            fib = fi[P16, q4 * 128:(q4 + 1) * 128].unsqueeze(2).to_broadcast([16, 128, 16])
            self.tt(v4(Bbr), v4(Bnr), frb, ALU.mult, [bB, bn], [bB])
            self.tt(v4(Btmp), v4(Bni), fib, ALU.mult, [bB, bn], [bB])
            self.tt(Bbr[P16, :], Bbr[P16, :], Btmp[P16, :], ALU.subtract, [bB], [bB])
            self.tt(v4(Bbi), v4(Bni), frb, ALU.mult, [bB, bn], [bB])
            self.tt(v4(Btmp), v4(Bnr), fib, ALU.mult, [bB, bn], [bB])
            self.tt(Bbi[P16, :], Bbi[P16, :], Btmp[P16, :], ALU.add, [bB], [bB])
            for (srcb, dstw, dstwB) in ((Bbr, WpR, WpBR), (Bbi, WpI, WpBI)):
                s4 = srcb[P16, :].rearrange("f (g p h) -> f p g h", g=2, h=16)
                if q4 < 3:
                    self.cp(Bp[P16, 0:2048].rearrange("f (p g h) -> f p g h", g=2, h=16), s4, [bB], [bBp])
                    base, nrow, dw, wcol = 32 * q4, 32, dstw, 32
                else:
                    self.memset(Bp[P16, :], 0.0, [bBp])
                    self.cp(Bp[P16, :].rearrange("f (p z g h) -> f p z g h", z=2, g=2, h=16)[:, :, 1, :, :], s4, [bB], [bBp])
                    base, nrow, dw, wcol = 64, 64, dstwB, 64
                for half in range(2):
                    pbk = self.PS[half]
                    for pp in range(32):
                        p = half * 32 + pp
                        self.mm(pbk[base:base + nrow, pp * 16:(pp + 1) * 16], Bp[P16, p * wcol:(p + 1) * wcol], self.identf[P16, P16],
                                True, True, [bBp, self.bC], [self.bPS[half]])
                    for r in range(2):
                        self.ts(dw[base:base + nrow, :, r * 64 + half * 32:r * 64 + half * 32 + 32],
                                pbk[base:base + nrow, :].rearrange("k (p f) -> k f p", f=16),
                                self.mB[base:base + nrow, r:r + 1], None, ALU.mult, None,
                                [self.bPS[half], self.bC], [bPar])
        S.barrier()
        self.a_reset(markB)
        Cn = a32(8192); bCn = Buf("Cn"); Cp = a32(8192); bCp = Buf("Cp")
        for (csrc, dstL, dstH, mcol) in ((self.cre, VcLR, VcHR, 0), (self.cim, VcLI, VcHI, 2)):
            S.dma("sp", Cn[P16, :], csrc[j].rearrange("(f g) h p -> f (g h p)", g=8), writes=[bCn])
            cv5 = Cn[P16, :].rearrange("f (q r h p) -> f q h r p", q=4, r=2, h=16)
            for q4 in range(4):
                self.cp(Cp[P16, q4 * 2048:(q4 + 1) * 2048].rearrange("f (h r p) -> f h r p", r=2, p=64), cv5[:, q4], [bCn], [bCp])
            for q4 in range(4):
                pbk = self.PS[q4 % 2]
                for hh in range(16):
                    self.tr(pbk[:, hh * 16:(hh + 1) * 16], Cp[P16, (q4 * 16 + hh) * 128:(q4 * 16 + hh + 1) * 128], self.identf[P16, P16],
                            [bCp, self.bC], [self.bPS[q4 % 2]])
                for r in range(2):
                    if q4 < 2:
                        dv = dstL.rearrange("p (f q) c -> p f q c", q=2)[:, :, q4, r * 16:(r + 1) * 16]
                    else:
                        co = (q4 - 2) * 32 + r * 16
                        dv = dstH.rearrange("p (f q) c -> p f q c", q=2)[:, :, q4 - 2, co:co + 16]
                    self.ts(dv, pbk[:, 0:256].rearrange("k (h f) -> k f h", f=16), self.mC[:, mcol + r:mcol + r + 1], None,
                            ALU.mult, None, [self.bPS[q4 % 2], self.bC], [bPar])
        if smp:
            st_ = a32(128)
            for (ssrc, dstk) in ((self.sre, K0r), (self.sim, K0i)):
                S.dma("sp", st_[0:64, :], ssrc[j].rearrange("(q r) p -> q (r p)", r=2), writes=[bn])
                self.tr(self.PS[0][:, 0:64], st_[0:64, :], self.identf[0:64, 0:64], [bn, self.bC], [self.bPS[0]])
                self.cp(dstk[:, :], self.PS[0][:, 0:64], [self.bPS[0]], [bK0])
        else:
            self.memset(K0r[:, :], 0.0, [bK0])
            self.memset(K0i[:, :], 0.0, [bK0])
        S.barrier()
        if "P" in self.dbg:
            return
        self.a_reset(mark)
        for half in range(2):
            S.dma("pool", Wout[:, half * 8:(half + 1) * 8, :],
                  self.wos[j, half * 1024:(half + 1) * 1024, :].rearrange("(kc k) c -> k kc c", k=128), writes=[bW])
        hT = a16(8 * tc).rearrange("p (k t) -> p k t", t=tc); bhT = Buf("hT")
        uT = a16(16 * tc).rearrange("p (k t) -> p k t", t=tc); buT = Buf("uT")
        YT = a16(16 * tc).rearrange("p (k t) -> p k t", t=tc); bYT = Buf("YT")
        MT = uT; bMT = buT
        wbi = [a16(8 * 128).rearrange("p (k c) -> p k c", c=128) for _ in range(2)]; bwbi = [Buf("wi0"), Buf("wi1")]
        wbg = [a16(16 * 128).rearrange("p (k c) -> p k c", c=128)] * 2; bwbg = [Buf("wg0")] * 2
        ss = a32(4); rs = a32(2); tmpo = a32(512); bscr = Buf("scr")
        markS = self.apos
        sq = a32(DM); hb = a16(DM)
        self.a_reset(markS)
        sct = min(tc, 256)
        ang = a32(sct); angi = self.ai32(sct); angf = a32(sct); cosT = a32(sct); sinT = a32(sct); bTab = Buf("tab")
        xr = a32(sct); xi = a32(sct); tA = a32(sct); tB = a32(sct); bx = Buf("x")
        kr = a32(sct); ki = a32(sct); bkk = Buf("k")
        hr = a16(sct); hi = a16(sct); bh = Buf("h")
        yv = a32(tc); sgm = a32(tc); slu = a32(tc); bY = Buf("yv")
        self.apos = max(self.apos, markS + 3072)
        hf = a32(4); tmp = tA
        wi = 0
        for ch in range(nch):
            c0 = ch * tc
            self.load_gain(self.npre[layer:layer + 1, :])
            for t_ in range(tc // tp):
                tt = ch * (tc // tp) + t_
                self.prenorm_tile(tt, hT, bhT, t_ * tp, (sq, ss, rs, hb), bscr)
            for fb in range(16):
                w = wbi[wi % 2]; bw = bwbi[wi % 2]; wi += 1
                self.load_wblk(w, bw, self.wis[j], fb * 128, 8)
                pbk = self.PS[fb % 2]; bp = self.bPS[fb % 2]
                for kc in range(8):
                    self.mm(pbk[:, 0:tc], w[:, kc, :], hT[:, kc, :], kc == 0, kc == 7, [bw, bhT], [bp])
                self.cp(uT[:, fb, :], pbk[:, 0:tc], [bp], [buT], eng="act")
            S.barrier()
            nsc = tc // sct
            for fb in range(16):
                ybk = self.PS[4]; by = self.bPS[4]
                for sc in range(nsc):
                    cs = slice(sc * sct, (sc + 1) * sct)
                    for q4 in range(4):
                        rho_c = RhoT[:, q4 * 16 + fb:q4 * 16 + fb + 1]
                        phi_c = PhiT[:, q4 * 16 + fb:q4 * 16 + fb + 1]
                        q = fb * 4 + q4
                        self.ts(ang[:, :], self.iotac[:, 0:sct], float(c0 + sc * sct), phi_c, ALU.add, ALU.mult, [self.bC, bPar], [bTab])
                        self.cp(angi[:, :], ang[:, :], [bTab], [bTab])
                        self.cp(angf[:, :], angi[:, :], [bTab], [bTab])
                        self.tt(ang[:, :], ang[:, :], angf[:, :], ALU.subtract, [bTab], [bTab])
                        self.act(sinT[:, :], ang[:, :], AF.Sin, [bTab, self.bC], [bTab], bias=self.cols[:, 3:4], scale=TWO_PI)
                        self.stt(angf[:, :], ang[:, :], -1.0, ang[:, :], ALU.mult, ALU.max, [bTab], [bTab])
                        self.act(cosT[:, :], angf[:, :], AF.Sin, [bTab, self.bC], [bTab], bias=self.cols[:, 2:3], scale=-TWO_PI)
                        pr = self.PS[2]; pi_ = self.PS[3]
                        if q4 < 2:
                            rows = slice(32 * q4, 32 * q4 + 32); wr_, wi_ = WpR, WpI
                        elif q4 == 2:
                            rows = slice(64, 128); wr_, wi_ = WpR, WpI
                        else:
                            rows = slice(64, 128); wr_, wi_ = WpBR, WpBI
                        urows = uT[rows, fb, cs]
                        self.mm(pr[:, 0:sct], wr_[rows, fb, :], urows, True, True, [bPar, buT], [self.bPS[2]])
                        self.mm(pi_[:, 0:sct], wi_[rows, fb, :], urows, True, True, [bPar, buT], [self.bPS[3]])
                        self.tt(xr[:, :], pr[:, 0:sct], cosT[:, :], ALU.mult, [self.bPS[2], bTab], [bx])
                        self.tt(tA[:, :], pi_[:, 0:sct], sinT[:, :], ALU.mult, [self.bPS[3], bTab], [bx])
                        self.tt(xr[:, :], xr[:, :], tA[:, :], ALU.add, [bx], [bx])
                        self.tt(xi[:, :], pi_[:, 0:sct], cosT[:, :], ALU.mult, [self.bPS[3], bTab], [bx])
                        self.tt(tB[:, :], pr[:, 0:sct], sinT[:, :], ALU.mult, [self.bPS[2], bTab], [bx])
                        self.tt(xi[:, :], xi[:, :], tB[:, :], ALU.subtract, [bx], [bx])
                        rb = rho_c.to_broadcast([128, sct])
                        self.scan(kr[:, :], rb, xr[:, :], K0r[:, q:q + 1], [bPar, bx, bK0], [bkk])
                        self.scan(ki[:, :], rb, xi[:, :], K0i[:, q:q + 1], [bPar, bx, bK0], [bkk])
                        self.cp(K0r[:, q:q + 1], kr[:, sct - 1:sct], [bkk], [bK0])
                        self.cp(K0i[:, q:q + 1], ki[:, sct - 1:sct], [bkk], [bK0])
                        lastc = (ch == nch - 1) and (sc == nsc - 1)
                        self.tt(tA[:, :], kr[:, :], cosT[:, :], ALU.mult, [bkk, bTab], [bx])
                        self.tt(tB[:, :], ki[:, :], sinT[:, :], ALU.mult, [bkk, bTab], [bx])
                        self.tt(hr[:, :], tA[:, :], tB[:, :], ALU.subtract, [bx], [bh])
                        if lastc:
                            self.tt(HFr[:, q:q + 1], tA[:, sct - 1:sct], tB[:, sct - 1:sct], ALU.subtract, [bx], [bHF])
                        self.tt(tA[:, :], kr[:, :], sinT[:, :], ALU.mult, [bkk, bTab], [bx])
                        self.tt(tB[:, :], ki[:, :], cosT[:, :], ALU.mult, [bkk, bTab], [bx])
                        self.tt(hi[:, :], tA[:, :], tB[:, :], ALU.add, [bx], [bh])
                        if lastc:
                            self.tt(HFi[:, q:q + 1], tA[:, sct - 1:sct], tB[:, sct - 1:sct], ALU.add, [bx], [bHF])
                        if q4 < 2:
                            qq = fb * 2 + q4
                            self.mm(ybk[32 * q4:32 * q4 + 32, cs], VcLR[:, qq, :], hr[:, :], True, False, [bPar, bh], [by])
                            self.mm(ybk[32 * q4:32 * q4 + 32, cs], VcLI[:, qq, :], hi[:, :], False, True, [bPar, bh], [by])
                        else:
                            qq = fb * 2 + q4 - 2
                            self.mm(ybk[64:128, cs], VcHR[:, qq, :], hr[:, :], q4 == 2, False, [bPar, bh], [by])
                            self.mm(ybk[64:128, cs], VcHI[:, qq, :], hi[:, :], False, q4 == 3, [bPar, bh], [by])
                self.stt(yv[:, :], uT[:, fb, :], DT[:, fb:fb + 1], ybk[:, 0:tc], ALU.mult, ALU.add, [buT, bPar, by], [bY])
                self.act(YT[:, fb, :], yv[:, :], AF.Gelu, [bY], [bYT])
            S.barrier()
            for jb in range(16):
                wg = wbg[jb % 2]; bwg = bwbg[jb % 2]
                self.load_wblk(wg, bwg, self.wgl[j], jb * 128, 16)
                w = wbi[wi % 2]; bw = bwbi[wi % 2]; wi += 1
                self.load_wblk(w, bw, self.wis[j], 2048 + jb * 128, 8)
                pg = self.PS[jb % 2]; bpg = self.bPS[jb % 2]
                pt = self.PS[2 + jb % 2]; bpt = self.bPS[2 + jb % 2]
                for kb in range(16):
                    self.mm(pg[:, 0:tc], wg[:, kb, :], YT[:, kb, :], kb == 0, kb == 15, [bwg, bYT], [bpg])
                for kc in range(8):
                    self.mm(pt[:, 0:tc], w[:, kc, :], hT[:, kc, :], kc == 0, kc == 7, [bw, bhT], [bpt])
                self.act(sgm[:, :], pg[:, 0:tc], AF.Sigmoid, [bpg], [bY])
                self.act(slu[:, :], pt[:, 0:tc], AF.Silu, [bpt], [bY])
                self.tt(sgm[:, :], sgm[:, :], slu[:, :], ALU.mult, [bY], [bY])
                self.tt(MT[:, jb, :], YT[:, jb, :], sgm[:, :], ALU.mult, [bYT, bY], [bMT])
            self.load_gain(self.npost[layer:layer + 1, :])
            for t_ in range(tc // tp):
                tt = ch * (tc // tp) + t_
                self.outproj_tile(tt, MT, bMT, t_ * tp, Wout, bW, (sq[:, 0:512], ss, rs, tmpo), bscr)
        sro = self.srs[j] if smp else self.srp[j, s]
        sio = self.sis[j] if smp else self.sip[j, s]
        fst = a32(128); bf_ = Buf("fst")
        for (srcH, dsto) in ((HFr, sro), (HFi, sio)):
            self.tr(self.PS[0][0:64, 0:128], srcH[:, :], self.identf[:, :], [bHF, self.bC], [self.bPS[0]])
            self.cp(fst[0:64, :], self.PS[0][0:64, 0:128], [self.bPS[0]], [bf_])
            S.dma("sp", dsto.rearrange("(q r) p -> q (r p)", r=2), fst[0:64, :], reads=[bf_])


_CACHE = {}


def _consts():
    ident = np.eye(128, dtype=np.float32)
    jj, ss = np.meshgrid(np.arange(128), np.arange(128), indexing="ij")
    negU = np.where(jj >= ss, -1.0, 0.0).astype(np.float32)
    s_, q_ = np.meshgrid(np.arange(128), np.arange(512), indexing="ij")
    maskP = np.concatenate([(128 * jb + s_ < q_).astype(np.float32) for jb in range(4)], axis=1)
    iota = np.broadcast_to(np.arange(1, 513, dtype=np.float32)[None, :], (128, 512)).copy()
    part = np.arange(128)
    mB = np.stack([((part // 16) % 2 == r) for r in range(2)], axis=1).astype(np.float32)
    mC0 = np.stack([(part // 64 == r) for r in range(2)], axis=1).astype(np.float32)
    mC = np.concatenate([mC0, -mC0], axis=1).astype(np.float32)
    return {"c_ident": ident, "c_negU": negU, "c_maskP": maskP, "c_iota": iota, "c_mB": mB, "c_mC": mC}


def kernel(x_prompt, x_sample, cache_sb_k, cache_sb_v, state_ssm_re, state_ssm_im,
           norm_pre, norm_post, w_in_ssm, ssm_a_re, ssm_a_im, ssm_log_step, ssm_b_re, ssm_b_im,
           ssm_c_re, ssm_c_im, ssm_d, w_glu, w_out_ssm, w_in_sb, w_out_sb):
    if "nc" not in _CACHE:
        _CACHE["nc"] = Prog().build()
    nc = _CACHE["nc"]
    f = lambda a: np.ascontiguousarray(np.asarray(a, dtype=np.float32))
    shared = {"npre": f(norm_pre), "npost": f(norm_post), "wis": f(w_in_ssm), "are": f(ssm_a_re), "aim": f(ssm_a_im),
              "lst": f(ssm_log_step), "bre": f(ssm_b_re), "bim": f(ssm_b_im), "cre": f(ssm_c_re), "cim": f(ssm_c_im),
              "dsk": f(ssm_d), "wgl": f(w_glu), "wos": f(w_out_ssm), "wib": f(w_in_sb), "wob": f(w_out_sb)}
    shared.update(_consts())
    in_maps = []
    for c in range(NCORES):
        m = dict(shared)
        m["xp"] = f(x_prompt[c * NPS:(c + 1) * NPS])
        m["xs"] = f(x_sample[c])
        m["ck"] = f(cache_sb_k[:, c])
        m["cv"] = f(cache_sb_v[:, c])
        m["sre"] = f(state_ssm_re[:, c])
        m["sim"] = f(state_ssm_im[:, c])
        in_maps.append(m)
    res = run_bass_kernel_spmd(nc, in_maps, core_ids=list(range(NCORES))).results
    cat = lambda k, ax: np.concatenate([r[k] for r in res], axis=ax)
    stk = lambda k, ax: np.stack([r[k] for r in res], axis=ax)
    y_prompt = cat("yp", 0)
    y_sample = stk("ys", 0)
    k_prompt = cat("kp", 1)
    v_prompt = cat("vp", 1)
    srp = cat("srp", 1)
    sip = cat("sip", 1)
    k_sample = stk("ks", 1)
    v_sample = stk("vs", 1)
    srs = stk("srs", 1)
    sis = stk("sis", 1)
    return (y_prompt, y_sample, k_prompt, v_prompt, srp, sip, k_sample, v_sample, srs, sis)
```

```python
import contextlib
import numpy as np
import ml_dtypes
import concourse.bass as bass
import concourse.mybir as mybir
from concourse.bass_utils import run_bass_kernel_spmd

F32 = mybir.dt.float32
BF16 = mybir.dt.bfloat16
I32 = mybir.dt.int32
ALU = mybir.AluOpType
AF = mybir.ActivationFunctionType

NCORES = 8
DM = 1024
TP = 2048
TS = 32
PAST = 4096
NPS = 4
EPS = 1e-6
TWO_PI = float(2 * np.pi)


class Buf:
    __slots__ = ("name", "lw", "rd", "x")

    def __init__(self, name="", x=False):
        self.name = name
        self.lw = None
        self.rd = {}
        self.x = x


class Sched:
    NSLOT = 6

    def __init__(self, nc, stack):
        self.nc = nc
        self.names = ["pe", "act", "dve", "pool", "sp"]
        self.prog = {e: [] for e in self.names}
        self.sem = {e: stack.enter_context(nc.semaphore("s_" + e)) for e in self.names}
        self.cnt = {e: 0 for e in self.names}
        self.seen = {e: {} for e in self.names}
        self.slots = {}
        for e in ["sp", "act", "pool"]:
            self.slots[e] = [[stack.enter_context(nc.semaphore("d_%s%d" % (e, i))), 0] for i in range(self.NSLOT)]
        self.slot_i = {e: 0 for e in self.slots}

    def _waits(self, eng, deps):
        out = []
        seen = self.seen[eng]
        for (sem, val) in deps:
            k = id(sem)
            if seen.get(k, 0) >= val:
                continue
            seen[k] = val
            out.append((sem, val))
        return out

    def _deps(self, eng, reads, writes):
        deps = []
        own = self.sem[eng]
        for b in reads:
            if b.lw is not None:
                deps.append(b.lw)
        for b in writes:
            if b.lw is not None:
                deps.append(b.lw)
            deps.extend(b.rd.values())
        if eng == "pe":
            deps = [d for d in deps if d[0] is not own]
        return deps

    def op(self, eng, fn, reads=(), writes=()):
        writes = list(writes) + [b for b in reads if b.x]
        reads = [b for b in reads if not b.x]
        waits = self._waits(eng, self._deps(eng, reads, writes))
        self.cnt[eng] += 1
        c = self.cnt[eng]
        sem = self.sem[eng]
        self.prog[eng].append((waits, fn, sem, 1))
        k = id(sem)
        for b in reads:
            if b.rd.get(k, (None, 0))[1] < c:
                b.rd[k] = (sem, c)
        for b in writes:
            b.lw = (sem, c)
            b.rd = {}

    def dma(self, eng, out, in_, reads=(), writes=()):
        slots = self.slots[eng]
        i = self.slot_i[eng]
        self.slot_i[eng] = (i + 1) % len(slots)
        sl = slots[i]
        deps = self._deps(eng, reads, writes)
        if sl[1] > 0:
            deps.append((sl[0], sl[1]))
        waits = self._waits(eng, deps)
        sl[1] += 16
        dsem, dval = sl[0], sl[1]
        self.prog[eng].append((waits, (lambda e, o=out, i_=in_: e.dma_start(out=o, in_=i_)), dsem, 16))
        for b in reads:
            b.rd[id(dsem)] = (dsem, dval)
        for b in writes:
            b.lw = (dsem, dval)
            b.rd = {}

    def barrier(self):
        deps = [(self.sem[e], self.cnt[e]) for e in self.names if self.cnt[e] > 0]
        for e, slots in self.slots.items():
            deps += [(s[0], s[1]) for s in slots if s[1] > 0]
        for e in self.names:
            own = self.sem[e]
            waits = self._waits(e, [d for d in deps if not (e == "pe" and d[0] is own)])
            if waits:
                self.prog[e].append((waits, None, None, 0))

    def finish(self):
        self.barrier()

    def emit(self, block):
        def run(engname):
            def f(e):
                for (waits, fn, sem, inc) in self.prog[engname]:
                    for (s, v) in waits:
                        e.wait_ge(s, v)
                    if fn is not None:
                        fn(e).then_inc(sem, inc)
            return f
        block.tensor(run("pe"))
        block.scalar(run("act"))
        block.vector(run("dve"))
        block.gpsimd(run("pool"))
        block.sync(run("sp"))


class Prog:
    def __init__(self, seqs=None):
        self.nc = bass.Bass("TRN2", target_bir_lowering=False)
        self.st = contextlib.ExitStack()
        self.seqs = list(range(NPS + 1)) if seqs is None else seqs

    def din(self, n, s):
        return self.nc.dram_tensor(n, list(s), F32, kind="ExternalInput").ap()

    def dout(self, n, s):
        return self.nc.dram_tensor(n, list(s), F32, kind="ExternalOutput").ap()

    def sb(self, n, s, dt=F32):
        return self.st.enter_context(self.nc.sbuf_tensor(n, list(s), dt))

    def dump(self, name, ap, bufs, shape):
        if "D" not in getattr(self, "dbg", ""):
            return
        if not hasattr(self, "dumps"):
            self.dumps = {}
        if name in self.dumps:
            return
        o = self.nc.dram_tensor("dbg_" + name, list(shape), F32, kind="ExternalOutput").ap()
        self.dumps[name] = o
        stg = self.sb("dstg_" + name, list(shape))
        b = Buf("dstg")
        self.cp(stg[:], ap, bufs, [b])
        self.S.dma("sp", o[:, :], stg[:], reads=[b])

    def a_reset(self, mark=0):
        self.apos = mark

    def a16(self, n):
        n = (n + 1) // 2 * 2
        ap = self.AR[:, self.apos:self.apos + n]
        self.apos += n
        assert self.apos <= self.NA, ("arena overflow", self.apos, self.NA)
        return ap

    def a32(self, n):
        return self.a16(2 * n).bitcast(F32)

    def ai32(self, n):
        return self.a16(2 * n).bitcast(I32)

    def mm(self, out, lhsT, rhs, start, stop, r, w):
        self.S.op("pe", lambda e: e.matmul(out, lhsT, rhs, start=start, stop=stop), r, w)

    def tr(self, out, in_, ident, r, w):
        self.S.op("pe", lambda e: e.transpose(out, in_, ident), r, w)

    def act(self, out, in_, func, r, w, bias=None, scale=None, accum=None):
        kw = {}
        if bias is not None:
            kw["bias"] = bias
        if scale is not None:
            kw["scale"] = scale
        if accum is not None:
            kw["accum_out"] = accum
        self.S.op("act", lambda e: e.activation(out, in_, func, **kw), r, w)

    def tt(self, out, a, b, op, r, w, eng="dve"):
        self.S.op(eng, lambda e: e.tensor_tensor(out, a, b, op), r, w)

    def ts(self, out, a, s1, s2, op0, op1, r, w, eng="dve"):
        if s2 is None:
            self.S.op(eng, lambda e: e.tensor_scalar(out, a, s1, None, op0), r, w)
        else:
            self.S.op(eng, lambda e: e.tensor_scalar(out, a, s1, s2, op0, op1), r, w)

    def stt(self, out, in0, scalar, in1, op0, op1, r, w):
        self.S.op("dve", lambda e: e.scalar_tensor_tensor(out, in0, scalar, in1, op0, op1), r, w)

    def cp(self, out, in_, r, w, eng="dve"):
        if eng == "act":
            self.S.op("act", lambda e: e.copy(out, in_), r, w)
        else:
            self.S.op(eng, lambda e: e.tensor_copy(out, in_), r, w)

    def scan(self, out, d0, d1, init, r, w):
        self.S.op("dve", lambda e: e.tensor_tensor_scan(out, d0, d1, init, ALU.mult, ALU.add), r, w)

    def recip(self, out, in_, r, w):
        self.S.op("dve", lambda e: e.reciprocal(out, in_), r, w)

    def memset(self, ap, v, w, eng="dve"):
        self.S.op(eng, lambda e: e.memset(ap, v), (), w)

    def build(self):
        nc = self.nc
        self.S = Sched(nc, self.st)
        S = self.S
        d = self.din
        self.xp = d("xp", [NPS, TP, DM]); self.xs = d("xs", [TS, DM])
        self.ck = d("ck", [2, 16, PAST, 128]); self.cv = d("cv", [2, 16, PAST, 128])
        self.sre = d("sre", [2, 128, 64]); self.sim = d("sim", [2, 128, 64])
        self.npre = d("npre", [4, DM]); self.npost = d("npost", [4, DM])
        self.wis = d("wis", [2, DM, 4096]); self.are = d("are", [2, 128, 64]); self.aim = d("aim", [2, 128, 64])
        self.lst = d("lst", [2, 128]); self.bre = d("bre", [2, 128, 64, 16]); self.bim = d("bim", [2, 128, 64, 16])
        self.cre = d("cre", [2, 128, 16, 64]); self.cim = d("cim", [2, 128, 16, 64]); self.dsk = d("dsk", [2, 2048])
        self.wgl = d("wgl", [2, 2048, 2048]); self.wos = d("wos", [2, 2048, DM])
        self.wib = d("wib", [2, DM, 8192]); self.wob = d("wob", [2, 2048, DM])
        c_ident = d("c_ident", [128, 128]); c_negU = d("c_negU", [128, 128]); c_maskP = d("c_maskP", [128, 2048])
        c_iota = d("c_iota", [128, 512]); c_mB = d("c_mB", [128, 2]); c_mC = d("c_mC", [128, 4])
        o = self.dout
        self.yp = o("yp", [NPS, TP, DM]); self.ys = o("ys", [TS, DM])
        self.kp = o("kp", [2, NPS, 16, TP, 128]); self.vp = o("vp", [2, NPS, 16, TP, 128])
        self.srp = o("srp", [2, NPS, 128, 64]); self.sip = o("sip", [2, NPS, 128, 64])
        self.ks = o("ks", [2, 16, TS, 128]); self.vs = o("vs", [2, 16, TS, 128])
        self.srs = o("srs", [2, 128, 64]); self.sis = o("sis", [2, 128, 64])

        sb = self.sb
        self.X = sb("X", [128, 16, DM]); self.bX = [Buf("X%d" % i) for i in range(16)]
        self.identf = sb("identf", [128, 128]); self.identb = sb("identb", [128, 128], BF16)
        self.negU = sb("negU", [128, 128], BF16); self.negOnes = sb("negOnes", [128, 128], BF16)
        self.maskP = sb("maskP", [128, 2048], BF16)
        self.G = sb("G", [128, DM]); self.bG = Buf("G")
        self.cols = sb("cols", [128, 8])
        self.mB = sb("mB", [128, 2]); self.mC = sb("mC", [128, 4])
        self.iotac = sb("iotac", [128, 512])
        self.NA = 67800
        self.AR = sb("arena", [128, self.NA], BF16)
        self.apos = 0
        self.bC = Buf("consts")
        self.PS = [self.st.enter_context(nc.psum_tensor("ps%d" % i, [128, 512], F32)) for i in range(8)]
        self.bPS = [Buf("ps%d" % i, x=True) for i in range(8)]

        tmpf = self.a32(2048)
        bt = Buf("tmpc")
        S.dma("sp", self.identf[:], c_ident[:, :], writes=[self.bC])
        S.dma("sp", self.iotac[:], c_iota[:, :], writes=[self.bC])
        S.dma("sp", self.mB[:], c_mB[:, :], writes=[self.bC])
        S.dma("sp", self.mC[:], c_mC[:, :], writes=[self.bC])
        S.dma("sp", tmpf[:, 0:128], c_negU[:, :], writes=[bt])
        self.cp(self.negU[:], tmpf[:, 0:128], [bt], [self.bC])
        self.cp(self.identb[:], self.identf[:], [self.bC], [self.bC])
        self.memset(self.negOnes[:], -1.0, [self.bC])
        self.memset(self.cols[:, 0:1], EPS, [self.bC])
        self.memset(self.cols[:, 1:2], 1.0, [self.bC])
        self.memset(self.cols[:, 2:3], float(np.pi / 2), [self.bC])
        self.memset(self.cols[:, 3:4], 0.0, [self.bC])
        S.dma("sp", tmpf[:, :], c_maskP[:, :], reads=[], writes=[bt])
        self.cp(self.maskP[:], tmpf[:, :], [bt], [self.bC])
        S.barrier()
        self.a_reset()

        for s in self.seqs:
            self.run_seq(s)
        S.finish()
        with nc.Block() as block:
            S.emit(block)
        return nc

    def run_seq(self, s):
        S = self.S
        smp = (s == NPS)
        T = TS if smp else TP
        tp = min(T, 128)
        ntt = T // tp
        self.T, self.tp, self.ntt, self.smp, self.s = T, tp, ntt, smp, s
        self.tc = min(T, 512)
        self.nch = T // self.tc
        src = self.xs if smp else self.xp[s]
        for tt in range(ntt):
            S.dma("sp", self.X[0:tp, tt, :], src[tt * tp:(tt + 1) * tp, :], writes=[self.bX[tt]])
        import os
        self.dbg = os.environ.get("K_DBG", "")
        nl = int(self.dbg[1]) if self.dbg.startswith("L") else 4
        for layer in range(nl):
            j = layer // 2
            if layer % 2 == 0:
                self.ssm_layer(layer, j)
            else:
                self.sb_layer(layer, j)
        dst = self.ys if smp else self.yp[s]
        for tt in range(ntt):
            S.dma("sp", dst[tt * tp:(tt + 1) * tp, :], self.X[0:tp, tt, :], reads=[self.bX[tt]])

    def load_gain(self, src_row):
        self.S.dma("sp", self.G[:], src_row.partition_broadcast(128), writes=[self.bG])

    def prenorm_tile(self, tt, HT, bHT, col0, scr, bscr):
        tp = self.tp
        sq, ss, rs, hb = scr
        xt = self.X[0:tp, tt, :]
        self.memset(ss[0:tp, 0:4], 0.0, [bscr])
        self.act(sq[0:tp, :], xt, AF.Square, [self.bX[tt]], [bscr], accum=ss[0:tp, 0:1])
        self.act(rs[0:tp, 0:1], ss[0:tp, 0:1], AF.Sqrt, [bscr, self.bC], [bscr], bias=self.cols[0:tp, 0:1], scale=1.0 / DM)
        self.recip(rs[0:tp, 1:2], rs[0:tp, 0:1], [bscr], [bscr])
        self.stt(hb[0:tp, :], xt, rs[0:tp, 1:2], self.G[0:tp, :], ALU.mult, ALU.mult, [self.bX[tt], bscr, self.bG], [bscr])
        pb = self.PS[7][:].bitcast(BF16)
        for kc in range(8):
            self.tr(pb[:, kc * 128:kc * 128 + tp], hb[0:tp, kc * 128:(kc + 1) * 128], self.identb[0:tp, 0:tp],
                    [bscr, self.bC], [self.bPS[7]])
        self.cp(HT[:, :, col0:col0 + tp], pb.rearrange("p (k c) -> p k c", c=128)[:, :, 0:tp], [self.bPS[7]], [bHT], eng="act")

    def outproj_tile(self, tt, ZT, bZT, col0, Wout, bW, scr, bscr):
        tp = self.tp
        sq, ss, rs, tmp = scr
        for half in range(2):
            pbk = self.PS[5 + half]
            for jb in range(16):
                self.mm(pbk[0:tp, :], ZT[:, jb, col0:col0 + tp], Wout[:, jb, half * 512:(half + 1) * 512],
                        jb == 0, jb == 15, [bZT, bW], [self.bPS[5 + half]])
        self.memset(ss[0:tp, 0:4], 0.0, [bscr])
        for half in range(2):
            self.act(sq[0:tp, :], self.PS[5 + half][0:tp, :], AF.Square, [self.bPS[5 + half]], [bscr], accum=ss[0:tp, half:half + 1])
        self.tt(ss[0:tp, 2:3], ss[0:tp, 0:1], ss[0:tp, 1:2], ALU.add, [bscr], [bscr])
        self.act(rs[0:tp, 0:1], ss[0:tp, 2:3], AF.Sqrt, [bscr, self.bC], [bscr], bias=self.cols[0:tp, 0:1], scale=1.0 / DM)
        self.recip(rs[0:tp, 1:2], rs[0:tp, 0:1], [bscr], [bscr])
        for half in range(2):
            xs_ = self.X[0:tp, tt, half * 512:(half + 1) * 512]
            self.stt(tmp[0:tp, :], self.PS[5 + half][0:tp, :], rs[0:tp, 1:2], self.G[0:tp, half * 512:(half + 1) * 512],
                     ALU.mult, ALU.mult, [self.bPS[5 + half], bscr, self.bG], [bscr])
            self.tt(xs_, xs_, tmp[0:tp, :], ALU.add, [bscr, self.bX[tt]], [self.bX[tt]])

    def load_wblk(self, dst, bdst, w2d, c0, kc):
        self.S.dma("pool", dst, w2d[:, c0:c0 + 128].rearrange("(kc k) c -> k kc c", k=128), writes=[bdst])

    def sb_layer(self, layer, j):
        S = self.S
        T, tp, ntt, tc, nch, smp, s = self.T, self.tp, self.ntt, self.tc, self.nch, self.smp, self.s
        S.barrier()
        self.a_reset()
        HT = self.a16(8 * T).rearrange("p (k t) -> p k t", t=T); bHT = Buf("HT")
        OT = self.a16(16 * T).rearrange("p (k t) -> p k t", t=T); bOT = Buf("OT")
        mark = self.apos
        sq = self.a32(DM); ss = self.a32(4); rs = self.a32(2); hb = self.a16(DM); bscr = Buf("scr")
        self.load_gain(self.npre[layer:layer + 1, :])
        for tt in range(ntt):
            self.prenorm_tile(tt, HT, bHT, tt * tp, (sq, ss, rs, hb), bscr)
        S.barrier()
        if "a" in self.dbg:
            return
        self.a_reset(mark)
        wb = [self.a16(8 * 128).rearrange("p (k c) -> p k c", c=128) for _ in range(2)]
        bwb = [Buf("wb0"), Buf("wb1")]
        qT = self.a16(T); kT = self.a16(T); SG = self.a16(T)
        V = self.a16(ntt * 128).rearrange("p (t d) -> p t d", d=128)
        bq, bk, bv, bsg = Buf("qT"), Buf("kT"), Buf("V"), Buf("SG")
        Kst = self.a32(512); Vst = self.a32(512); Kb = self.a16(512)
        bKst, bVst, bKb = Buf("Kst"), Buf("Vst"), Buf("Kb")
        E = self.a32(512); bE = Buf("E")
        SP = [self.a16(512), self.a16(512)]; bSP = [Buf("SP0"), Buf("SP1")]
        SS = self.a16(512); bSS = Buf("SS")
        A = [self.a16(512), self.a16(512)]; bA = [Buf("A0"), Buf("A1")]
        if smp:
            KC = [self.a16(128), self.a16(128)]; bKC = [Buf("KC0"), Buf("KC1")]
            VC = [self.a16(128), self.a16(128)]; bVC = [Buf("VC0"), Buf("VC1")]
            KCT = [self.a16(128), self.a16(128)]; bKCT = [Buf("KCT0"), Buf("KCT1")]
        w2d = self.wib[j]
        kdst = self.ks[j] if smp else self.kp[j, s]
        vdst = self.vs[j] if smp else self.vp[j, s]
        wi = 0
        ntg = (ntt + 3) // 4
        for h in range(16):
            cols = [h * 128, 2048 + h * 128, 4096 + h * 128, 6144 + h * 128]
            for which, c0 in ((0, cols[0]), (3, cols[3])):
                w = wb[wi % 2]; bw = bwb[wi % 2]; wi += 1
                self.load_wblk(w, bw, w2d, c0, 8)
                for ch in range(nch):
                    pbk = self.PS[ch % 2]; bp = self.bPS[ch % 2]
                    for kc in range(8):
                        self.mm(pbk[:, 0:tc], w[:, kc, :], HT[:, kc, ch * tc:(ch + 1) * tc], kc == 0, kc == 7, [bw, bHT], [bp])
                    if which == 0:
                        self.act(qT[:, ch * tc:(ch + 1) * tc], pbk[:, 0:tc], AF.Copy, [bp], [bq], scale=float(128 ** -0.5))
                    else:
                        self.act(SG[:, ch * tc:(ch + 1) * tc], pbk[:, 0:tc], AF.Silu, [bp], [bsg])
            for which, c0 in ((1, cols[1]), (2, cols[2])):
                w = wb[wi % 2]; bw = bwb[wi % 2]; wi += 1
                self.load_wblk(w, bw, w2d, c0, 8)
                for tg in range(ntg):
                    nt_ = min(4, ntt - tg * 4)
                    pbk = self.PS[tg % 2]; bp = self.bPS[tg % 2]
                    for ti in range(nt_):
                        tt = tg * 4 + ti
                        for kc in range(8):
                            self.mm(pbk[0:tp, ti * 128:(ti + 1) * 128], HT[:, kc, tt * tp:(tt + 1) * tp], w[:, kc, :],
                                    kc == 0, kc == 7, [bw, bHT], [bp])
                    ncol = nt_ * 128
                    if which == 1:
                        self.cp(Kst[0:tp, 0:ncol], pbk[0:tp, 0:ncol], [bp], [bKst])
                        self.cp(Kb[0:tp, 0:ncol], pbk[0:tp, 0:ncol], [bp], [bKb], eng="act")
                        S.dma("sp", kdst[h, tg * 4 * tp:(tg * 4 + nt_) * tp, :].rearrange("(t p) d -> p t d", p=tp),
                              Kst[0:tp, 0:ncol].rearrange("p (t d) -> p t d", d=128), reads=[bKst])
                        pb = self.PS[7][:].bitcast(BF16)
                        for ti in range(nt_):
                            self.tr(pb[:, ti * 128:ti * 128 + tp], Kb[0:tp, ti * 128:(ti + 1) * 128], self.identb[0:tp, 0:tp],
                                    [bKb, self.bC], [self.bPS[7]])
                        if tp == 128:
                            self.cp(kT[:, tg * 512:tg * 512 + ncol], pb[:, 0:ncol], [self.bPS[7]], [bk])
                        else:
                            self.cp(kT[:, 0:tp], pb[:, 0:tp], [self.bPS[7]], [bk])
                    else:
                        self.cp(Vst[0:tp, 0:ncol], pbk[0:tp, 0:ncol], [bp], [bVst])
                        self.cp(V[0:tp, tg * 4:tg * 4 + nt_, :], pbk[0:tp, 0:ncol].rearrange("p (t d) -> p t d", d=128), [bp], [bv], eng="act")
                        S.dma("sp", vdst[h, tg * 4 * tp:(tg * 4 + nt_) * tp, :].rearrange("(t p) d -> p t d", p=tp),
                              Vst[0:tp, 0:ncol].rearrange("p (t d) -> p t d", d=128), reads=[bVst])
            if "b" in self.dbg:
                continue
            nqt = nch
            for qi in range(nqt):
                nq = tc
                qs = qT[:, qi * tc:(qi + 1) * tc]
                if smp:
                    blocks = [("cur", 0)] + [("past", b) for b in range(PAST // 128 - 1, -1, -1)]
                else:
                    blocks = [("cur", b) for b in range(4 * qi + 3, -1, -1)]
                acc = self.PS[4 + (qi % 2)] if not smp else self.PS[4]
                bacc = self.bPS[4 + (qi % 2)] if not smp else self.bPS[4]
                nb = len(blocks)
                for bi, (kind, b) in enumerate(blocks):
                    zb = self.PS[2 + bi % 2]; bz = self.bPS[2 + bi % 2]
                    sp_, bsp = SP[bi % 2], bSP[bi % 2]
                    a_, ba = A[bi % 2], bA[bi % 2]
                    mask = None
                    if kind == "past":
                        sl = bi % 2
                        S.dma("pool", KC[sl][:, :], self.ck[j, h, b * 128:(b + 1) * 128, :], writes=[bKC[sl]])
                        S.dma("pool", VC[sl][:, :], self.cv[j, h, b * 128:(b + 1) * 128, :], writes=[bVC[sl]])
                        pb = self.PS[7][:].bitcast(BF16)
                        self.tr(pb[:, 0:128], KC[sl][:, :], self.identb[:, :], [bKC[sl], self.bC], [self.bPS[7]])
                        self.cp(KCT[sl][:, :], pb[:, 0:128], [self.bPS[7]], [bKCT[sl]])
                        nk = 128
                        kTb = KCT[sl][:, :]; rk = [bKCT[sl]]
                        vb = VC[sl][:, :]; rv = [bVC[sl]]
                    else:
                        nk = tp
                        kTb = kT[:, b * tp:(b + 1) * tp]; rk = [bk]
                        vb = V[0:tp, b, :]; rv = [bv]
                        if smp:
                            mask = self.maskP[0:nk, 0:nq]
                        elif b >= 4 * qi:
                            mask = self.maskP[0:nk, (b - 4 * qi) * 512:(b - 4 * qi + 1) * 512]
                    self.mm(zb[0:nk, 0:nq], kTb, qs, True, True, rk + [bq], [bz])
                    self.act(E[0:nk, 0:nq], zb[0:nk, 0:nq], AF.Exp, [bz], [bE])
                    self.act(sp_[0:nk, 0:nq], E[0:nk, 0:nq], AF.Ln, [bE, self.bC], [bsp], bias=self.cols[0:nk, 1:2])
                    if mask is not None:
                        self.tt(sp_[0:nk, 0:nq], sp_[0:nk, 0:nq], mask, ALU.mult, [bsp, self.bC], [bsp])
                    last_acc = (bi == 0)
                    self.mm(zb[0:nk, 0:nq], self.negU[0:nk, 0:nk], sp_[0:nk, 0:nq], False, last_acc, [bsp, self.bC], [bz])
                    if bi > 0:
                        self.mm(zb[0:nk, 0:nq], self.negOnes[0:128, 0:nk], SS[:, 0:nq], False, True, [bSS, self.bC], [bz])
                    self.act(a_[0:nk, 0:nq], zb[0:nk, 0:nq], AF.Exp, [bz], [ba])
                    if mask is not None:
                        self.tt(a_[0:nk, 0:nq], a_[0:nk, 0:nq], mask, ALU.mult, [ba, self.bC], [ba])
                    self.mm(acc[:, 0:nq], vb, a_[0:nk, 0:nq], bi == 0, bi == nb - 1, rv + [ba], [bacc])
                    if bi < nb - 1:
                        if bi == 0:
                            if nk < 128:
                                self.memset(SS[:, 0:nq], 0.0, [bSS])
                            self.cp(SS[0:nk, 0:nq], sp_[0:nk, 0:nq], [bsp], [bSS])
                        else:
                            self.tt(SS[0:nk, 0:nq], SS[0:nk, 0:nq], sp_[0:nk, 0:nq], ALU.add, [bsp, bSS], [bSS])
                if h == 0 and qi == 0:
                    self.dump("acc", acc[:, 0:32], [bacc], [128, 32])
                    self.dump("qT", qT[:, 0:32], [bq], [128, 32])
                    self.dump("kT", kT[:, 0:32], [bk], [128, 32])
                    self.dump("SG", SG[:, 0:32], [bsg], [128, 32])
                    self.dump("SS", SS[:, 0:32], [bSS], [128, 32])
                    self.dump("A", A[0][:, 0:32], [bA[0]], [128, 32])
                    self.dump("SP", SP[0][:, 0:32], [bSP[0]], [128, 32])
                    self.dump("E", E[:, 0:32], [bE], [128, 32])
                self.tt(OT[:, h, qi * tc:(qi + 1) * tc], acc[:, 0:nq], SG[:, qi * tc:(qi + 1) * tc], ALU.mult, [bacc, bsg], [bOT])
        S.barrier()
        if "c" in self.dbg or "b" in self.dbg:
            return
        self.a_reset(0 if 8 * T >= 2100 else mark)
        sq = self.a32(512); ss = self.a32(4); rs = self.a32(2); tmp = self.a32(512); bscr = Buf("scr2")
        self.a_reset(max(mark, self.apos) if 8 * T < 2100 else mark)
        Wout = self.a16(16 * DM).rearrange("p (k c) -> p k c", c=DM); bW = Buf("Wout")
        for half in range(2):
            S.dma("pool", Wout[:, half * 8:(half + 1) * 8, :],
                  self.wob[j, half * 1024:(half + 1) * 1024, :].rearrange("(kc k) c -> k kc c", k=128), writes=[bW])
        self.load_gain(self.npost[layer:layer + 1, :])
        for tt in range(ntt):
            self.outproj_tile(tt, OT, bOT, tt * tp, Wout, bW, (sq, ss, rs, tmp), bscr)

    def ssm_layer(self, layer, j):
        S = self.S
        T, tp, ntt, tc, nch, smp, s = self.T, self.tp, self.ntt, self.tc, self.nch, self.smp, self.s
        S.barrier()
        self.a_reset()
        a16, a32 = self.a16, self.a32
        w3 = lambda n, c: a16(n * c).rearrange("p (f c) -> p f c", c=c)
        WpR = w3(16, 128); WpI = w3(16, 128); WpBR = w3(16, 128); WpBI = w3(16, 128)
        VcLR = w3(32, 32); VcLI = w3(32, 32); VcHR = w3(32, 64); VcHI = w3(32, 64)
        RhoT = a32(64); PhiT = a32(64); DT = a32(16)
        K0r = a32(64); K0i = a32(64); HFr = a32(64); HFi = a32(64)
        bPar = Buf("par"); bK0 = Buf("K0"); bHF = Buf("HF")
        markW = self.apos
        Wout = a16(16 * DM).rearrange("p (k c) -> p k c", c=DM); bW = Buf("Wout")
        mark = self.apos
        self.a_reset(markW)
        for wz in (WpR, WpI, WpBR, WpBI):
            self.memset(wz[64:128, :, :], 0.0, [bPar])
        for wz in (VcHR, VcHI):
            self.memset(wz[:, :, :], 0.0, [bPar])
        NPn = 512
        ar = a32(NPn); ai_ = a32(NPn); dtb = a32(8); rho = a32(NPn); phi = a32(NPn)
        t0 = a32(NPn); t1 = a32(NPn); t2 = a32(NPn); t3 = a32(NPn); t4 = a32(NPn); ti = self.ai32(NPn)
        fr = a32(NPn); fi = a32(NPn)
        bn = Buf("prel")
        P16 = slice(0, 16)
        S.dma("sp", ar[P16, :], self.are[j].rearrange("(f g) p -> f (g p)", g=8), writes=[bn])
        S.dma("sp", ai_[P16, :], self.aim[j].rearrange("(f g) p -> f (g p)", g=8), writes=[bn])
        S.dma("sp", dtb[P16, :], self.lst[j].rearrange("(f g) -> f g", g=8), writes=[bn])
        self.act(dtb[P16, :], dtb[P16, :], AF.Exp, [bn], [bn])
        dtbb = dtb[P16, :].unsqueeze(2).to_broadcast([16, 8, 64])
        v3 = lambda ap: ap[P16, :].rearrange("f (g p) -> f g p", p=64)
        self.tt(v3(t0), v3(ar), dtbb, ALU.mult, [bn], [bn])
        self.act(rho[P16, :], t0[P16, :], AF.Exp, [bn], [bn])
        self.tt(v3(t0), v3(ai_), dtbb, ALU.mult, [bn], [bn])
        self.ts(phi[P16, :], t0[P16, :], float(1.0 / TWO_PI), None, ALU.mult, None, [bn], [bn])
        self.cp(ti[P16, :], phi[P16, :], [bn], [bn])
        self.cp(t1[P16, :], ti[P16, :], [bn], [bn])
        self.tt(t1[P16, :], phi[P16, :], t1[P16, :], ALU.subtract, [bn], [bn])
        self.act(t2[P16, :], t1[P16, :], AF.Sin, [bn, self.bC], [bn], bias=self.cols[P16, 3:4], scale=TWO_PI)
        self.stt(t3[P16, :], t1[P16, :], -1.0, t1[P16, :], ALU.mult, ALU.max, [bn], [bn])
        self.act(t3[P16, :], t3[P16, :], AF.Sin, [bn, self.bC], [bn], bias=self.cols[P16, 2:3], scale=-TWO_PI)
        self.tt(t2[P16, :], t2[P16, :], rho[P16, :], ALU.mult, [bn], [bn])
        self.tt(t3[P16, :], t3[P16, :], rho[P16, :], ALU.mult, [bn], [bn])
        self.ts(t3[P16, :], t3[P16, :], -1.0, None, ALU.add, None, [bn], [bn])
        self.tt(t0[P16, :], ar[P16, :], ar[P16, :], ALU.mult, [bn], [bn])
        self.tt(t1[P16, :], ai_[P16, :], ai_[P16, :], ALU.mult, [bn], [bn])
        self.tt(t0[P16, :], t0[P16, :], t1[P16, :], ALU.add, [bn], [bn])
        self.recip(t0[P16, :], t0[P16, :], [bn], [bn])
        self.tt(t1[P16, :], t3[P16, :], ar[P16, :], ALU.mult, [bn], [bn])
        self.tt(t4[P16, :], t2[P16, :], ai_[P16, :], ALU.mult, [bn], [bn])
        self.tt(t1[P16, :], t1[P16, :], t4[P16, :], ALU.add, [bn], [bn])
        self.tt(fr[P16, :], t1[P16, :], t0[P16, :], ALU.mult, [bn], [bn])
        self.tt(t1[P16, :], t2[P16, :], ar[P16, :], ALU.mult, [bn], [bn])
        self.tt(t4[P16, :], t3[P16, :], ai_[P16, :], ALU.mult, [bn], [bn])
        self.tt(t1[P16, :], t1[P16, :], t4[P16, :], ALU.subtract, [bn], [bn])
        self.tt(fi[P16, :], t1[P16, :], t0[P16, :], ALU.mult, [bn], [bn])
        for (srct, dstt) in ((rho, RhoT), (phi, PhiT)):
            for q4 in range(4):
                self.tr(self.PS[0][:, q4 * 16:(q4 + 1) * 16], srct[P16, q4 * 128:(q4 + 1) * 128], self.identf[P16, P16],
                        [bn, self.bC], [self.bPS[0]])
            self.cp(dstt[:, :], self.PS[0][:, 0:64], [self.bPS[0]], [bPar])
        dn = a32(128)
        S.dma("sp", dn[P16, :], self.dsk[j].rearrange("(f c) -> f c", c=128), writes=[bn])
        self.tr(self.PS[0][:, 0:16], dn[P16, :], self.identf[P16, P16], [bn, self.bC], [self.bPS[0]])
        self.cp(DT[:, :], self.PS[0][:, 0:16], [self.bPS[0]], [bPar])
        markB = self.apos
        Bnr = a32(2048); Bni = a32(2048); Btmp = a32(2048); Bbr = a32(2048); Bbi = a32(2048); Bp = a32(4096)
        bB = Buf("Bn"); bBp = Buf("Bp")
        brv = self.bre[j].rearrange("(f g) p h -> f g (p h)", g=8)
        biv = self.bim[j].rearrange("(f g) p h -> f g (p h)", g=8)
        for q4 in range(4):
            S.dma("sp", Bnr[P16, :].rearrange("f (g x) -> f g x", g=2), brv[:, 2 * q4:2 * q4 + 2, :], writes=[bB])
            S.dma("sp", Bni[P16, :].rearrange("f (g x) -> f g x", g=2), biv[:, 2 * q4:2 * q4 + 2, :], writes=[bB])
            v4 = lambda ap: ap[P16, :].rearrange("f (g p h) -> f (g p) h", g=2, h=16)
            frb = fr[P16, q4 * 128:(q4 + 1) * 128].unsqueeze(2).to_broadcast([16, 128, 16])
            fib = fi[P16, q4 * 128:(q4 + 1) * 128].unsqueeze(2).to_broadcast([16, 128, 16])
            self.tt(v4(Bbr), v4(Bnr), frb, ALU.mult, [bB, bn], [bB])
            self.tt(v4(Btmp), v4(Bni), fib, ALU.mult, [bB, bn], [bB])
            self.tt(Bbr[P16, :], Bbr[P16, :], Btmp[P16, :], ALU.subtract, [bB], [bB])
            self.tt(v4(Bbi), v4(Bni), frb, ALU.mult, [bB, bn], [bB])
            self.tt(v4(Btmp), v4(Bnr), fib, ALU.mult, [bB, bn], [bB])
            self.tt(Bbi[P16, :], Bbi[P16, :], Btmp[P16, :], ALU.add, [bB], [bB])
            for (srcb, dstw, dstwB) in ((Bbr, WpR, WpBR), (Bbi, WpI, WpBI)):
                s4 = srcb[P16, :].rearrange("f (g p h) -> f p g h", g=2, h=16)
                if q4 < 3:
                    self.cp(Bp[P16, 0:2048].rearrange("f (p g h) -> f p g h", g=2, h=16), s4, [bB], [bBp])
                    base, nrow, dw, wcol = 32 * q4, 32, dstw, 32
                else:
                    self.memset(Bp[P16, :], 0.0, [bBp])
                    self.cp(Bp[P16, :].rearrange("f (p z g h) -> f p z g h", z=2, g=2, h=16)[:, :, 1, :, :], s4, [bB], [bBp])
                    base, nrow, dw, wcol = 64, 64, dstwB, 64
                for half in range(2):
                    pbk = self.PS[half]
                    for pp in range(32):
                        p = half * 32 + pp
                        self.mm(pbk[base:base + nrow, pp * 16:(pp + 1) * 16], Bp[P16, p * wcol:(p + 1) * wcol], self.identf[P16, P16],
                                True, True, [bBp, self.bC], [self.bPS[half]])
                    for r in range(2):
                        self.ts(dw[base:base + nrow, :, r * 64 + half * 32:r * 64 + half * 32 + 32],
                                pbk[base:base + nrow, :].rearrange("k (p f) -> k f p", f=16),
                                self.mB[base:base + nrow, r:r + 1], None, ALU.mult, None,
                                [self.bPS[half], self.bC], [bPar])
        S.barrier()
        self.a_reset(markB)
        Cn = a32(8192); bCn = Buf("Cn"); Cp = a32(8192); bCp = Buf("Cp")
        for (csrc, dstL, dstH, mcol) in ((self.cre, VcLR, VcHR, 0), (self.cim, VcLI, VcHI, 2)):
            S.dma("sp", Cn[P16, :], csrc[j].rearrange("(f g) h p -> f (g h p)", g=8), writes=[bCn])
            cv5 = Cn[P16, :].rearrange("f (q r h p) -> f q h r p", q=4, r=2, h=16)
            for q4 in range(4):
                self.cp(Cp[P16, q4 * 2048:(q4 + 1) * 2048].rearrange("f (h r p) -> f h r p", r=2, p=64), cv5[:, q4], [bCn], [bCp])
            for q4 in range(4):
                pbk = self.PS[q4 % 2]
                for hh in range(16):
                    self.tr(pbk[:, hh * 16:(hh + 1) * 16], Cp[P16, (q4 * 16 + hh) * 128:(q4 * 16 + hh + 1) * 128], self.identf[P16, P16],
                            [bCp, self.bC], [self.bPS[q4 % 2]])
                for r in range(2):
                    if q4 < 2:
                        dv = dstL.rearrange("p (f q) c -> p f q c", q=2)[:, :, q4, r * 16:(r + 1) * 16]
                    else:
                        co = (q4 - 2) * 32 + r * 16
                        dv = dstH.rearrange("p (f q) c -> p f q c", q=2)[:, :, q4 - 2, co:co + 16]
                    self.ts(dv, pbk[:, 0:256].rearrange("k (h f) -> k f h", f=16), self.mC[:, mcol + r:mcol + r + 1], None,
                            ALU.mult, None, [self.bPS[q4 % 2], self.bC], [bPar])
        if smp:
            st_ = a32(128)
            for (ssrc, dstk) in ((self.sre, K0r), (self.sim, K0i)):
                S.dma("sp", st_[0:64, :], ssrc[j].rearrange("(q r) p -> q (r p)", r=2), writes=[bn])
                self.tr(self.PS[0][:, 0:64], st_[0:64, :], self.identf[0:64, 0:64], [bn, self.bC], [self.bPS[0]])
                self.cp(dstk[:, :], self.PS[0][:, 0:64], [self.bPS[0]], [bK0])
        else:
            self.memset(K0r[:, :], 0.0, [bK0])
            self.memset(K0i[:, :], 0.0, [bK0])
        S.barrier()
        if "P" in self.dbg:
            return
        self.a_reset(mark)
        for half in range(2):
            S.dma("pool", Wout[:, half * 8:(half + 1) * 8, :],
                  self.wos[j, half * 1024:(half + 1) * 1024, :].rearrange("(kc k) c -> k kc c", k=128), writes=[bW])
        hT = a16(8 * tc).rearrange("p (k t) -> p k t", t=tc); bhT = Buf("hT")
        uT = a16(16 * tc).rearrange("p (k t) -> p k t", t=tc); buT = Buf("uT")
        YT = a16(16 * tc).rearrange("p (k t) -> p k t", t=tc); bYT = Buf("YT")
        MT = uT; bMT = buT
        wbi = [a16(8 * 128).rearrange("p (k c) -> p k c", c=128)] * 2; bwbi = [Buf("wi0")] * 2
        wbg = [a16(16 * 128).rearrange("p (k c) -> p k c", c=128)] * 2; bwbg = [Buf("wg0")] * 2
        ss = a32(4); rs = a32(2); bscr = Buf("scr")
        markS = self.apos
        sq = a32(DM); hb = a16(DM)
        self.a_reset(markS)
        sct = min(tc, 512)
        ang = a32(sct); angi = self.ai32(sct); angf = a32(sct); cosT = a32(sct); sinT = a32(sct); bTab = Buf("tab")
        xr = a32(sct); xi = a32(sct); tA = a32(sct); tB = a32(sct); bx = Buf("x")
        kr = a32(sct); ki = a32(sct); bkk = Buf("k")
        hr = a16(sct); hi = a16(sct); bh = Buf("h")
        yv = xr; bY = bx; sgm = kr; slu = ki
        tmpo = xi if sct >= 512 else a32(512)
        self.apos = max(self.apos, markS + 3072)
        hf = a32(4); tmp = tA
        wi = 0
        for ch in range(nch):
            c0 = ch * tc
            self.load_gain(self.npre[layer:layer + 1, :])
            for t_ in range(tc // tp):
                tt = ch * (tc // tp) + t_
                self.prenorm_tile(tt, hT, bhT, t_ * tp, (sq, ss, rs, hb), bscr)
            for fb in range(16):
                w = wbi[wi % 2]; bw = bwbi[wi % 2]; wi += 1
                self.load_wblk(w, bw, self.wis[j], fb * 128, 8)
                pbk = self.PS[fb % 2]; bp = self.bPS[fb % 2]
                for kc in range(8):
                    self.mm(pbk[:, 0:tc], w[:, kc, :], hT[:, kc, :], kc == 0, kc == 7, [bw, bhT], [bp])
                self.cp(uT[:, fb, :], pbk[:, 0:tc], [bp], [buT], eng="act")
            S.barrier()
            nsc = tc // sct
            for fb in range(16):
                ybk = self.PS[4]; by = self.bPS[4]
                for sc in range(nsc):
                    cs = slice(sc * sct, (sc + 1) * sct)
                    for q4 in range(4):
                        rho_c = RhoT[:, q4 * 16 + fb:q4 * 16 + fb + 1]
                        phi_c = PhiT[:, q4 * 16 + fb:q4 * 16 + fb + 1]
                        q = fb * 4 + q4
                        self.ts(ang[:, :], self.iotac[:, 0:sct], float(c0 + sc * sct), phi_c, ALU.add, ALU.mult, [self.bC, bPar], [bTab])
                        self.cp(angi[:, :], ang[:, :], [bTab], [bTab])
                        self.cp(angf[:, :], angi[:, :], [bTab], [bTab])
                        self.tt(ang[:, :], ang[:, :], angf[:, :], ALU.subtract, [bTab], [bTab])
                        self.act(sinT[:, :], ang[:, :], AF.Sin, [bTab, self.bC], [bTab], bias=self.cols[:, 3:4], scale=TWO_PI)
                        self.stt(angf[:, :], ang[:, :], -1.0, ang[:, :], ALU.mult, ALU.max, [bTab], [bTab])
                        self.act(cosT[:, :], angf[:, :], AF.Sin, [bTab, self.bC], [bTab], bias=self.cols[:, 2:3], scale=-TWO_PI)
                        pr = self.PS[2]; pi_ = self.PS[3]
                        if q4 < 2:
                            rows = slice(32 * q4, 32 * q4 + 32); wr_, wi_ = WpR, WpI
                        elif q4 == 2:
                            rows = slice(64, 128); wr_, wi_ = WpR, WpI
                        else:
                            rows = slice(64, 128); wr_, wi_ = WpBR, WpBI
                        urows = uT[rows, fb, cs]
                        self.mm(pr[:, 0:sct], wr_[rows, fb, :], urows, True, True, [bPar, buT], [self.bPS[2]])
                        self.mm(pi_[:, 0:sct], wi_[rows, fb, :], urows, True, True, [bPar, buT], [self.bPS[3]])
                        self.tt(xr[:, :], pr[:, 0:sct], cosT[:, :], ALU.mult, [self.bPS[2], bTab], [bx])
                        self.tt(tA[:, :], pi_[:, 0:sct], sinT[:, :], ALU.mult, [self.bPS[3], bTab], [bx])
                        self.tt(xr[:, :], xr[:, :], tA[:, :], ALU.add, [bx], [bx])
                        self.tt(xi[:, :], pi_[:, 0:sct], cosT[:, :], ALU.mult, [self.bPS[3], bTab], [bx])
                        self.tt(tB[:, :], pr[:, 0:sct], sinT[:, :], ALU.mult, [self.bPS[2], bTab], [bx])
                        self.tt(xi[:, :], xi[:, :], tB[:, :], ALU.subtract, [bx], [bx])
                        rb = rho_c.to_broadcast([128, sct])
                        self.scan(kr[:, :], rb, xr[:, :], K0r[:, q:q + 1], [bPar, bx, bK0], [bkk])
                        self.scan(ki[:, :], rb, xi[:, :], K0i[:, q:q + 1], [bPar, bx, bK0], [bkk])
                        self.cp(K0r[:, q:q + 1], kr[:, sct - 1:sct], [bkk], [bK0])
                        self.cp(K0i[:, q:q + 1], ki[:, sct - 1:sct], [bkk], [bK0])
                        lastc = (ch == nch - 1) and (sc == nsc - 1)
                        self.tt(tA[:, :], kr[:, :], cosT[:, :], ALU.mult, [bkk, bTab], [bx])
                        self.tt(tB[:, :], ki[:, :], sinT[:, :], ALU.mult, [bkk, bTab], [bx])
                        self.tt(hr[:, :], tA[:, :], tB[:, :], ALU.subtract, [bx], [bh])
                        if lastc:
                            self.tt(HFr[:, q:q + 1], tA[:, sct - 1:sct], tB[:, sct - 1:sct], ALU.subtract, [bx], [bHF])
                        self.tt(tA[:, :], kr[:, :], sinT[:, :], ALU.mult, [bkk, bTab], [bx])
                        self.tt(tB[:, :], ki[:, :], cosT[:, :], ALU.mult, [bkk, bTab], [bx])
                        self.tt(hi[:, :], tA[:, :], tB[:, :], ALU.add, [bx], [bh])
                        if lastc:
                            self.tt(HFi[:, q:q + 1], tA[:, sct - 1:sct], tB[:, sct - 1:sct], ALU.add, [bx], [bHF])
                        if q4 < 2:
                            qq = fb * 2 + q4
                            self.mm(ybk[32 * q4:32 * q4 + 32, cs], VcLR[:, qq, :], hr[:, :], True, False, [bPar, bh], [by])
                            self.mm(ybk[32 * q4:32 * q4 + 32, cs], VcLI[:, qq, :], hi[:, :], False, True, [bPar, bh], [by])
                        else:
                            qq = fb * 2 + q4 - 2
                            self.mm(ybk[64:128, cs], VcHR[:, qq, :], hr[:, :], q4 == 2, False, [bPar, bh], [by])
                            self.mm(ybk[64:128, cs], VcHI[:, qq, :], hi[:, :], False, q4 == 3, [bPar, bh], [by])
                self.stt(yv[:, :], uT[:, fb, :], DT[:, fb:fb + 1], ybk[:, 0:tc], ALU.mult, ALU.add, [buT, bPar, by], [bY])
                self.act(YT[:, fb, :], yv[:, :], AF.Gelu, [bY], [bYT])
            S.barrier()
            for jb in range(16):
                wg = wbg[jb % 2]; bwg = bwbg[jb % 2]
                self.load_wblk(wg, bwg, self.wgl[j], jb * 128, 16)
                w = wbi[wi % 2]; bw = bwbi[wi % 2]; wi += 1
                self.load_wblk(w, bw, self.wis[j], 2048 + jb * 128, 8)
                pg = self.PS[jb % 2]; bpg = self.bPS[jb % 2]
                pt = self.PS[2 + jb % 2]; bpt = self.bPS[2 + jb % 2]
                for kb in range(16):
                    self.mm(pg[:, 0:tc], wg[:, kb, :], YT[:, kb, :], kb == 0, kb == 15, [bwg, bYT], [bpg])
                for kc in range(8):
                    self.mm(pt[:, 0:tc], w[:, kc, :], hT[:, kc, :], kc == 0, kc == 7, [bw, bhT], [bpt])
                self.act(sgm[:, :], pg[:, 0:tc], AF.Sigmoid, [bpg], [bY])
                self.act(slu[:, :], pt[:, 0:tc], AF.Silu, [bpt], [bY])
                self.tt(sgm[:, :], sgm[:, :], slu[:, :], ALU.mult, [bY], [bY])
                self.tt(MT[:, jb, :], YT[:, jb, :], sgm[:, :], ALU.mult, [bYT, bY], [bMT])
            self.load_gain(self.npost[layer:layer + 1, :])
            for t_ in range(tc // tp):
                tt = ch * (tc // tp) + t_
                self.outproj_tile(tt, MT, bMT, t_ * tp, Wout, bW, (sq[:, 0:512], ss, rs, tmpo), bscr)
        sro = self.srs[j] if smp else self.srp[j, s]
        sio = self.sis[j] if smp else self.sip[j, s]
        fst = a32(128); bf_ = Buf("fst")
        for (srcH, dsto) in ((HFr, sro), (HFi, sio)):
            self.tr(self.PS[0][0:64, 0:128], srcH[:, :], self.identf[:, :], [bHF, self.bC], [self.bPS[0]])
            self.cp(fst[0:64, :], self.PS[0][0:64, 0:128], [self.bPS[0]], [bf_])
            S.dma("sp", dsto.rearrange("(q r) p -> q (r p)", r=2), fst[0:64, :], reads=[bf_])


_CACHE = {}


def _consts():
    ident = np.eye(128, dtype=np.float32)
    jj, ss = np.meshgrid(np.arange(128), np.arange(128), indexing="ij")
    negU = np.where(jj >= ss, -1.0, 0.0).astype(np.float32)
    s_, q_ = np.meshgrid(np.arange(128), np.arange(512), indexing="ij")
    maskP = np.concatenate([(128 * jb + s_ < q_).astype(np.float32) for jb in range(4)], axis=1)
    iota = np.broadcast_to(np.arange(1, 513, dtype=np.float32)[None, :], (128, 512)).copy()
    part = np.arange(128)
    mB = np.stack([((part // 16) % 2 == r) for r in range(2)], axis=1).astype(np.float32)
    mC0 = np.stack([(part // 64 == r) for r in range(2)], axis=1).astype(np.float32)
    mC = np.concatenate([mC0, -mC0], axis=1).astype(np.float32)
    return {"c_ident": ident, "c_negU": negU, "c_maskP": maskP, "c_iota": iota, "c_mB": mB, "c_mC": mC}


def kernel(x_prompt, x_sample, cache_sb_k, cache_sb_v, state_ssm_re, state_ssm_im,
           norm_pre, norm_post, w_in_ssm, ssm_a_re, ssm_a_im, ssm_log_step, ssm_b_re, ssm_b_im,
           ssm_c_re, ssm_c_im, ssm_d, w_glu, w_out_ssm, w_in_sb, w_out_sb):
    if "nc" not in _CACHE:
        _CACHE["nc"] = Prog().build()
    nc = _CACHE["nc"]
    f = lambda a: np.ascontiguousarray(np.asarray(a, dtype=np.float32))
    shared = {"npre": f(norm_pre), "npost": f(norm_post), "wis": f(w_in_ssm), "are": f(ssm_a_re), "aim": f(ssm_a_im),
              "lst": f(ssm_log_step), "bre": f(ssm_b_re), "bim": f(ssm_b_im), "cre": f(ssm_c_re), "cim": f(ssm_c_im),
              "dsk": f(ssm_d), "wgl": f(w_glu), "wos": f(w_out_ssm), "wib": f(w_in_sb), "wob": f(w_out_sb)}
    shared.update(_consts())
    in_maps = []
    for c in range(NCORES):
        m = dict(shared)
        m["xp"] = f(x_prompt[c * NPS:(c + 1) * NPS])
        m["xs"] = f(x_sample[c])
        m["ck"] = f(cache_sb_k[:, c])
        m["cv"] = f(cache_sb_v[:, c])
        m["sre"] = f(state_ssm_re[:, c])
        m["sim"] = f(state_ssm_im[:, c])
        in_maps.append(m)
    res = run_bass_kernel_spmd(nc, in_maps, core_ids=list(range(NCORES))).results
    cat = lambda k, ax: np.concatenate([r[k] for r in res], axis=ax)
    stk = lambda k, ax: np.stack([r[k] for r in res], axis=ax)
    y_prompt = cat("yp", 0)
    y_sample = stk("ys", 0)
    k_prompt = cat("kp", 1)
    v_prompt = cat("vp", 1)
    srp = cat("srp", 1)
    sip = cat("sip", 1)
    k_sample = stk("ks", 1)
    v_sample = stk("vs", 1)
    srs = stk("srs", 1)
    sis = stk("sis", 1)
    return (y_prompt, y_sample, k_prompt, v_prompt, srp, sip, k_sample, v_sample, srs, sis)
```
